# Optimizing a Trainium2 kernel written in Bass

```python
import jax
import jax.numpy as jnp
from jax import lax
import numpy as np

D_MODEL = 2048
BATCH = 8
SEQ = 2048
DEPTH = 1

CHUNK = 64
EPS = 1e-6
HALF_STEP = 0.5
D_FF = 5632
CONV_DIM = 1024
CONV_WIDTH = 31
RET_HEADS = 4
RET_QK_DIM = 256
RET_V_DIM = 512
RET_QK = RET_HEADS * RET_QK_DIM
RET_V = RET_HEADS * RET_V_DIM
ROPE_BASE = 10000.0
N_MOD_LAYER = 9
IN_COLS = 2 * CONV_DIM + 2 * RET_QK + 2 * RET_V + 2 * D_MODEL

kernel_name = "chunk_causal_conv_retention_hybrid"


def _split_cols(u, sizes):
    idx = np.cumsum(sizes)[:-1].tolist()
    return jnp.split(u, idx, axis=-1)


def rms_norm(x, g):
    xf = x.astype(jnp.float32)
    y = xf * lax.rsqrt(jnp.mean(xf * xf, axis=-1, keepdims=True) + EPS)
    return (y * g.astype(jnp.float32)).astype(x.dtype)


def layer_norm(x, g, b):
    xf = x.astype(jnp.float32)
    mu = jnp.mean(xf, axis=-1, keepdims=True)
    var = jnp.mean(jnp.square(xf - mu), axis=-1, keepdims=True)
    y = (xf - mu) * lax.rsqrt(var + EPS)
    return (y * g.astype(jnp.float32) + b.astype(jnp.float32)).astype(x.dtype)


def modulate(h, shift, scale):
    return h * (1.0 + scale[:, None, :]) + shift[:, None, :]


def swiglu(h, w1, w3, w2):
    return (jax.nn.silu(h @ w1) * (h @ w3)) @ w2


def conformer_conv(u, dw_w, dw_b, ln_g, ln_b, pw_w):
    a, b = jnp.split(u, 2, axis=-1)
    y = a * jax.nn.sigmoid(b)
    y = lax.conv_general_dilated(
        y, dw_w[:, None, :].astype(y.dtype), window_strides=(1,),
        padding=[(CONV_WIDTH - 1, 0)],
        dimension_numbers=('NWC', 'WIO', 'NWC'),
        feature_group_count=CONV_DIM) + dw_b
    y = jax.nn.silu(layer_norm(y, ln_g, ln_b))
    return y @ pw_w


def rotary(x, cos, sin):
    x1, x2 = jnp.split(x, 2, axis=-1)
    return jnp.concatenate([x1 * cos - x2 * sin, x1 * sin + x2 * cos], axis=-1)


def retention(q, k, v, g, gn_g, gn_b, w_o):
    B, S = q.shape[0], q.shape[1]
    n_chunks = S // CHUNK
    dt = q.dtype
    q = q.reshape(B, S, RET_HEADS, RET_QK_DIM)
    k = k.reshape(B, S, RET_HEADS, RET_QK_DIM)
    v = v.reshape(B, S, RET_HEADS, RET_V_DIM)
    pos = jnp.arange(S, dtype=jnp.float32)
    inv_freq = ROPE_BASE ** (-jnp.arange(0, RET_QK_DIM, 2, dtype=jnp.float32) / RET_QK_DIM)
    ang = pos[:, None] * inv_freq[None, :]
    cos = jnp.cos(ang)[:, None, :].astype(dt)
    sin = jnp.sin(ang)[:, None, :].astype(dt)
    q = rotary(q, cos, sin) * (RET_QK_DIM ** -0.5)
    k = rotary(k, cos, sin)
    log_gamma = jnp.log(1.0 - 2.0 ** (-5.0 - jnp.arange(RET_HEADS, dtype=jnp.float32)))
    idx = jnp.arange(CHUNK, dtype=jnp.float32)
    dist = jnp.abs(idx[:, None] - idx[None, :])
    intra = jnp.exp(dist[None] * log_gamma[:, None, None]).astype(dt)
    q_dec = jnp.exp((idx + 1.0)[:, None] * log_gamma[None, :]).astype(dt)
    k_dec = jnp.exp((CHUNK - 1.0 - idx)[:, None] * log_gamma[None, :]).astype(dt)
    chunk_dec = jnp.exp(CHUNK * log_gamma).astype(dt)
    qc = q.reshape(B, n_chunks, CHUNK, RET_HEADS, RET_QK_DIM)
    kc = k.reshape(B, n_chunks, CHUNK, RET_HEADS, RET_QK_DIM)
    vc = v.reshape(B, n_chunks, CHUNK, RET_HEADS, RET_V_DIM)
    scores = jnp.einsum('bnchd,bnmhd->bnhcm', qc, kc) * intra[None, None]
    o_intra = jnp.einsum('bnhcm,bnmhe->bnche', scores, vc)
    def step(R, inp):
        qn, kn, vn = inp
        o = jnp.einsum('bchd,bhde->bche', qn * q_dec[None, :, :, None], R)
        R = R * chunk_dec[None, :, None, None] + jnp.einsum(
            'bchd,bche->bhde', kn * k_dec[None, :, :, None], vn)
        return R, o
    R0 = jnp.zeros((B, RET_HEADS, RET_QK_DIM, RET_V_DIM), dtype=v.dtype)
    xs = (jnp.moveaxis(qc, 1, 0), jnp.moveaxis(kc, 1, 0), jnp.moveaxis(vc, 1, 0))
    _, o_cross = lax.scan(step, R0, xs)
    o = (o_intra + jnp.moveaxis(o_cross, 0, 1)).reshape(B, S, RET_HEADS, RET_V_DIM)
    of = o.astype(jnp.float32)
    mu = jnp.mean(of, axis=-1, keepdims=True)
    var = jnp.mean(jnp.square(of - mu), axis=-1, keepdims=True)
    of = ((of - mu) * lax.rsqrt(var + EPS)).reshape(B, S, RET_V)
    o = (of * gn_g.astype(jnp.float32) + gn_b.astype(jnp.float32)).astype(dt)
    return (jax.nn.silu(g) * o) @ w_o


def hybrid_mixer(h, w_in, dw_w, dw_b, ln_g, ln_b, pw_w, gn_g, gn_b, ret_w_o, w_out):
    u = h @ w_in
    u_conv, q, k, v, g, gate_c, gate_r = _split_cols(
        u, [2 * CONV_DIM, RET_QK, RET_QK, RET_V, RET_V, D_MODEL, D_MODEL])
    y_conv = conformer_conv(u_conv, dw_w, dw_b, ln_g, ln_b, pw_w)
    y_ret = retention(q, k, v, g, gn_g, gn_b, ret_w_o)
    merged = jax.nn.sigmoid(gate_c) * y_conv + jax.nn.sigmoid(gate_r) * y_ret
    return merged @ w_out


def setup_inputs(seed: int = 0) -> dict:
    key = jax.random.key(seed)
    ks = iter(jax.random.split(key, 32))
    L, D = DEPTH, D_MODEL

    def w(shape, fan_in):
        return jax.random.normal(next(ks), shape, jnp.float32) * (fan_in ** -0.5)

    def gain(shape):
        return 1.0 + 0.05 * jax.random.normal(next(ks), shape, jnp.float32)

    def bias(shape):
        return 0.02 * jax.random.normal(next(ks), shape, jnp.float32)

    return {
        "x": jax.random.normal(next(ks), (BATCH, SEQ, D), jnp.float32),
        "c": jax.random.normal(next(ks), (BATCH, D), jnp.float32),
        "ada_w": w((L, D, N_MOD_LAYER * D), D),
        "ada_b": bias((L, N_MOD_LAYER * D)),
        "ffn1_norm": gain((L, D)),
        "ffn1_w1": w((L, D, D_FF), D),
        "ffn1_w3": w((L, D, D_FF), D),
        "ffn1_w2": w((L, D_FF, D), D_FF),
        "mix_norm": gain((L, D)),
        "w_in": w((L, D, IN_COLS), D),
        "conv_dw_w": w((L, CONV_WIDTH, CONV_DIM), CONV_WIDTH),
        "conv_dw_b": bias((L, CONV_DIM)),
        "conv_ln_g": gain((L, CONV_DIM)),
        "conv_ln_b": bias((L, CONV_DIM)),
        "conv_pw_w": w((L, CONV_DIM, D), CONV_DIM),
        "ret_gn_g": gain((L, RET_V)),
        "ret_gn_b": bias((L, RET_V)),
        "ret_w_o": w((L, RET_V, D), RET_V),
        "w_out": w((L, D, D), D),
        "ffn2_norm": gain((L, D)),
        "ffn2_w1": w((L, D, D_FF), D),
        "ffn2_w3": w((L, D, D_FF), D),
        "ffn2_w2": w((L, D_FF, D), D_FF),
        "final_norm": gain((D,)),
        "ada_f_w": w((D, 2 * D), D),
        "ada_f_b": bias((2 * D,)),
    }


def reference(x, c, ada_w, ada_b, ffn1_norm, ffn1_w1, ffn1_w3, ffn1_w2, mix_norm, w_in,
              conv_dw_w, conv_dw_b, conv_ln_g, conv_ln_b, conv_pw_w, ret_gn_g, ret_gn_b,
              ret_w_o, w_out, ffn2_norm, ffn2_w1, ffn2_w3, ffn2_w2, final_norm,
              ada_f_w, ada_f_b):
    c_act = jax.nn.silu(c)
    for l in range(DEPTH):
        mod = c_act @ ada_w[l] + ada_b[l]
        sh1, sc1, g1, sh2, sc2, g2, sh3, sc3, g3 = jnp.split(mod, N_MOD_LAYER, axis=-1)
        h = modulate(rms_norm(x, ffn1_norm[l]), sh1, sc1)
        x = x + HALF_STEP * g1[:, None, :] * swiglu(h, ffn1_w1[l], ffn1_w3[l], ffn1_w2[l])
        h = modulate(rms_norm(x, mix_norm[l]), sh2, sc2)
        x = x + g2[:, None, :] * hybrid_mixer(
            h, w_in[l], conv_dw_w[l], conv_dw_b[l], conv_ln_g[l], conv_ln_b[l],
            conv_pw_w[l], ret_gn_g[l], ret_gn_b[l], ret_w_o[l], w_out[l])
        h = modulate(rms_norm(x, ffn2_norm[l]), sh3, sc3)
        x = x + HALF_STEP * g3[:, None, :] * swiglu(h, ffn2_w1[l], ffn2_w3[l], ffn2_w2[l])
    f = c_act @ ada_f_w + ada_f_b
    sh_f, sc_f = jnp.split(f, 2, axis=-1)
    return modulate(rms_norm(x, final_norm), sh_f, sc_f)
```

```python
import numpy as np
import concourse.bass as bass
import concourse.mybir as mybir
from concourse.bass_utils import run_bass_kernel_spmd

F32 = mybir.dt.float32
BF16 = mybir.dt.bfloat16
AF = mybir.ActivationFunctionType
ALU = mybir.AluOpType

D = 2048
S = 2048
T = 512
NT = S // T
DFF = 5632
NFF = DFF // 128
EPS = 1e-6
N_CORES = 8
STOP = None
DBG_TILES = None
DBG_NBLK = 4
DBG_NOACT = False
DBG_VS = 8
DBG_DUMPS = 0
USE_SCRATCH = True
DBG_VB = 4


class Prog:
    GR = 256

    def __init__(self):
        self.ops = []
        self.lw = {}
        self.rd = {}

    def _gr(self, reg):
        sp, lo, hi = reg
        gr = 2048 if sp == 'PS' else self.GR
        return [(sp, g) for g in range(lo // gr, (hi - 1) // gr + 1)]

    def op(self, eng, fn, reads=(), writes=(), dma=False, small=False):
        idx = len(self.ops)
        deps = {}

        def add(d):
            if d is None or d == idx:
                return
            o = self.ops[d]
            key = ('dma', d) if o['dma'] else o['eng']
            if deps.get(key, -1) < d:
                deps[key] = d

        rg = [g for r in reads for g in self._gr(r)]
        wg = [g for w in writes for g in self._gr(w)]
        for g in rg:
            add(self.lw.get(g))
        for g in wg:
            add(self.lw.get(g))
            for d in self.rd.get(g, {}).values():
                add(d)
        key = ('dma', idx) if dma else eng
        for g in rg:
            self.rd.setdefault(g, {})[key] = idx
        for g in wg:
            self.lw[g] = idx
            self.rd[g] = {}
        self.ops.append(dict(eng=eng, fn=fn, deps=deps, dma=dma, small=small))
        return idx

    def emit(self, nc, block, sems, dma_rings):
        ops = self.ops
        NS = len(next(iter(dma_rings.values())))
        dcount = {q: 0 for q in dma_rings}
        for o in ops:
            if o['dma']:
                q = o['eng']
                n = dcount[q]
                dcount[q] += 1
                o['dsem'] = dma_rings[q][n % NS]
                o['dval'] = 16 * (n // NS + 1)
        is_ms = [False] * len(ops)
        for o in ops:
            for key, d in o['deps'].items():
                if isinstance(key, tuple):
                    continue
                if (not o['dma']) and key == o['eng'] and not ops[d]['small']:
                    continue
                is_ms[d] = True
        mcount = {}
        for i, o in enumerate(ops):
            if is_ms[i]:
                mcount[o['eng']] = mcount.get(o['eng'], 0) + 1
                o['ms'] = mcount[o['eng']]
        by_eng = {}
        for i, o in enumerate(ops):
            by_eng.setdefault(o['eng'], []).append(i)
        final_waits = {q: [] for q in dma_rings}
        for q in dma_rings:
            last = {}
            for o in ops:
                if o['dma'] and o['eng'] == q:
                    last[id(o['dsem'])] = (o['dsem'], o['dval'])
            final_waits[q] = list(last.values())

        def run_engine(name, e):
            waited = {}

            def wait(sem, val):
                k = id(sem)
                if waited.get(k, 0) >= val:
                    return
                waited[k] = val
                e.wait_ge(sem, val)

            for i in by_eng.get(name, []):
                o = ops[i]
                for key, d in o['deps'].items():
                    od = ops[d]
                    if isinstance(key, tuple):
                        wait(od['dsem'], od['dval'])
                    else:
                        if (not o['dma']) and key == name and not od['small']:
                            continue
                        wait(sems[key], od['ms'])
                if o['dma'] and o['dval'] > 16:
                    wait(o['dsem'], o['dval'] - 16)
                o['dbg_waits'] = [(k, ops[d].get('ms', ops[d].get('dval')), d) for k, d in o['deps'].items()
                                  if isinstance(k, tuple) or o['dma'] or k != name or ops[d]['small']]
                ins = o['fn'](e)
                if o['dma']:
                    ins.then_inc(o['dsem'], 16)
                elif is_ms[i]:
                    ins.then_inc(sems[name], 1)
            if name in final_waits:
                for sem, val in final_waits[name]:
                    wait(sem, val)

        @block.tensor
        def _(e):
            run_engine('pe', e)

        @block.scalar
        def _(e):
            run_engine('act', e)

        @block.vector
        def _(e):
            run_engine('dve', e)

        @block.gpsimd
        def _(e):
            run_engine('pool', e)

        @block.sync
        def _(e):
            run_engine('sp', e)


class Buf:
    def __init__(self, name, t, el):
        self.name, self.t, self.el = name, t, el

    def __call__(self, a, b, p0=0, p1=128):
        return self.t[p0:p1, a:b], (self.name, a * self.el, b * self.el)


def build_program(n_tiles=NT, stop=None):
    nc = bass.Bass("TRN2", target_bir_lowering=False)
    dt = nc.dram_tensor
    xT = dt("xT", [D, S], F32, kind="ExternalInput").ap()
    cvec = dt("cvec", [128, 16], F32, kind="ExternalInput").ap()
    adab = dt("adab", [128, 176], F32, kind="ExternalInput").ap()
    vecs = dt("vecs", [128, 96], F32, kind="ExternalInput").ap()
    convw = dt("convw", [128, 248], F32, kind="ExternalInput").ap()
    convv = dt("convv", [128, 24], F32, kind="ExternalInput").ap()
    cst = dt("cst", [128, 520], F32, kind="ExternalInput").ap()
    identd = dt("identd", [128, 128], F32, kind="ExternalInput").ap()
    cosT = dt("cosT", [128, S], F32, kind="ExternalInput").ap()
    sinT = dt("sinT", [128, S], F32, kind="ExternalInput").ap()
    adaw = dt("adaw", [88, 128, 4096], F32, kind="ExternalInput").ap()
    w13 = [dt("w13_%d" % k, [44, 128, 4096], F32, kind="ExternalInput").ap() for k in range(2)]
    w2 = [dt("w2_%d" % k, [16, 128, 5632], F32, kind="ExternalInput").ap() for k in range(2)]
    win = dt("win", [48, 128, 4096], F32, kind="ExternalInput").ap()
    pw = dt("pw", [8, 128, 2048], F32, kind="ExternalInput").ap()
    wo = dt("wo", [8, 128, 4096], F32, kind="ExternalInput").ap()
    wout = dt("wout", [8, 128, 4096], F32, kind="ExternalInput").ap()
    outT = dt("outT", [D, S], F32, kind="ExternalOutput").ap()
    SCRW = {}
    for nm, shp in (("w13_0", [44, 128, 4096]), ("w13_1", [44, 128, 4096]), ("w2_0", [16, 128, 5632]), ("w2_1", [16, 128, 5632]),
                    ("win", [48, 128, 4096]), ("pw", [8, 128, 2048]), ("wo", [8, 128, 4096]), ("wout", [8, 128, 4096])):
        SCRW[nm] = dt("s_" + nm, shp, BF16, kind="Internal").ap()
    WSRC = {"w13_0": w13[0], "w13_1": w13[1], "w2_0": w2[0], "w2_1": w2[1], "win": win, "pw": pw, "wo": wo, "wout": wout}
    cur_tile = [0]
    WB_TILE = {"w13_0": 0, "w2_0": 0, "win": 0, "w13_1": 1, "w2_1": 1, "pw": 1, "wo": 1, "wout": 1}

    SLAB_BYTES = 33792
    NCS = 1700
    from contextlib import ExitStack
    with ExitStack() as es:
        def sb(name, n, dtype):
            return es.enter_context(nc.sbuf_tensor(name, [128, n], dtype))
        X = Buf('X', sb("X", 8192, F32), 4)
        H = Buf('H', sb("H", 8192, BF16), 2)
        R32 = Buf('R32', sb("R32", 4096, F32), 4)
        RBF = Buf('RBF', sb("RBF", 4096, BF16), 2)
        TAB = Buf('TAB', sb("TAB", 1024, F32), 4)
        SL = Buf('SL', sb("SL", SLAB_BYTES // 2, BF16), 2)
        SCRt = sb("SCR", 16384, F32)
        SCRF = Buf('SCR', SCRt, 4)
        YB = Buf('YB', sb("YB", 2 * 544, F32), 4)
        TMP = Buf('TMP', sb("TMP", 6 * 512, F32), 4)
        ST = Buf('ST', sb("ST", 3 * 512, F32), 4)
        CS = Buf('CS', sb("CS", NCS, F32), 4)
        CB = Buf('CB', sb("CB", 1024, BF16), 2)
        PSt = es.enter_context(nc.psum_tensor("PS", [128, 8 * 512], F32))
        PS = Buf('PS', PSt, 4)
        sems = {k: es.enter_context(nc.semaphore("s_" + k)) for k in ('pe', 'act', 'dve', 'pool', 'sp')}
        NSR = 6
        rings = {q: [es.enter_context(nc.semaphore("d_%s%d" % (q, i))) for i in range(NSR)] for q in ('pool', 'sp')}
        block = es.enter_context(nc.Block())

        P = Prog()

        def scr_bf(byte0, n):
            ap = SCRt[:, byte0 // 4:(byte0 + 2 * n) // 4].bitcast(BF16)
            return ap, ('SCR', byte0, byte0 + 2 * n)

        def scr_f(byte0, n):
            return SCRF(byte0 // 4, byte0 // 4 + n)

        def bank(b, a=0, n=512):
            return PS(b * 512 + a, b * 512 + a + n)

        o = [0]

        def cs_alloc(n):
            a = o[0]
            o[0] += n
            return a
        C_C = cs_alloc(16)
        C_ADAB = cs_alloc(176)
        C_MOD = cs_alloc(176)
        C_A1 = cs_alloc(16); C_G1 = cs_alloc(16); C_A2 = cs_alloc(16)
        C_A3 = cs_alloc(16); C_G3 = cs_alloc(16); C_AF = cs_alloc(16)
        C_VEC = cs_alloc(96)
        C_CW = cs_alloc(248)
        C_CV = cs_alloc(24)
        C_CST = cs_alloc(520)
        C_HALO = cs_alloc(240)
        C_EPS = cs_alloc(1)
        C_SM = cs_alloc(4 * 16)
        assert o[0] <= NCS
        C_MASK = C_CST
        C_KDEC = C_CST + 512
        C_EPSP = C_CST + 516
        B_ID, B_OD, B_OC, B_CA, B_ST = 0, 128, 256, 384, 512

        def dma(q, out, in_, reads, writes):
            P.op(q, lambda e, out=out, in_=in_: e.dma_start(out=out, in_=in_), reads=reads, writes=writes, dma=True)

        def load_w(nm, idx, nelem, so):
            a_, d_ = SL(so, so + nelem)
            dr = ('DR_' + nm, idx * nelem * 2, (idx + 1) * nelem * 2)
            wb = WB_TILE[nm] if USE_SCRATCH else 10 ** 9
            if cur_tile[0] <= wb:
                dma('pool', a_, WSRC[nm][idx, :, :], [], [d_])
                if cur_tile[0] == wb:
                    dma('sp', SCRW[nm][idx, :, :], a_, [d_], [dr])
            else:
                dma('pool', a_, SCRW[nm][idx, :, :], [dr], [d_])

        def small_load(dst_off, src, n):
            a, d = CS(dst_off, dst_off + n)
            dma('sp', a, src, [], [d])

        small_load(C_C, cvec[:, :], 16)
        small_load(C_ADAB, adab[:, :], 176)
        small_load(C_VEC, vecs[:, :], 96)
        small_load(C_CW, convw[:, :], 248)
        small_load(C_CV, convv[:, :], 24)
        small_load(C_CST, cst[:, :], 520)
        a, d = CB(B_ID, B_ID + 128)
        dma('pool', a, identd[:, :], [], [d])
        a, d = CB(B_OD, B_OD + 128)
        P.op('dve', lambda e, a=a: e.memset(a, 1.0 / D), writes=[d])
        a, d = CB(B_OC, B_OC + 128)
        P.op('dve', lambda e, a=a: e.memset(a, 1.0 / 1024), writes=[d])
        a, d = CS(C_HALO, C_HALO + 240)
        P.op('dve', lambda e, a=a: e.memset(a, 0.0), writes=[d])
        a, d = CS(C_EPS, C_EPS + 1)
        P.op('dve', lambda e, a=a: e.memset(a, EPS), writes=[d])
        a, d = R32(0, 4096)
        P.op('dve', lambda e, a=a: e.memset(a, 0.0), writes=[d])
        a, d = RBF(0, 4096)
        P.op('dve', lambda e, a=a: e.memset(a, 0.0), writes=[d])
        epsA, epsD = CS(C_EPS, C_EPS + 1)

        ca, cd = CS(C_C, C_C + 16)
        cba, cbd = CB(B_CA, B_CA + 16)
        P.op('act', lambda e: e.activation(out=cba, in_=ca, func=AF.Silu), reads=[cd], writes=[cbd], small=True)

        slab_ctr = [0]

        def slab8():
            k = slab_ctr[0] % 4
            slab_ctr[0] += 1
            return k * 4096

        slab11_ctr = [0]

        def slab11():
            k = slab11_ctr[0] % 3
            slab11_ctr[0] += 1
            return k * 5632

        def ada_slab(s):
            so = slab8()
            sa, sd = SL(so, so + 4096)
            dma('pool', sa, adaw[s, :, :], [], [sd])
            for sub in range(2):
                col = 2 * s + sub
                pa, pd = bank(7, col, 1)
                for kc in range(16):
                    la, ld = SL(so + kc * 256 + sub * 128, so + kc * 256 + sub * 128 + 128)
                    ra, rd = CB(B_CA + kc, B_CA + kc + 1)
                    P.op('pe', lambda e, pa=pa, la=la, ra=ra, kc=kc: e.matmul(pa, la, ra, start=(kc == 0), stop=(kc == 15)),
                         reads=[ld, rd], writes=[pd])

        def mod_finalize(c0, c1):
            pa, pd = bank(7, c0, c1 - c0)
            ba, bd = CS(C_ADAB + c0, C_ADAB + c1)
            ma, md = CS(C_MOD + c0, C_MOD + c1)
            P.op('dve', lambda e: e.tensor_tensor(out=ma, in0=pa, in1=ba, op=ALU.add), reads=[pd, bd], writes=[md], small=True)
        ada_next = [0]
        for s in range(24):
            ada_slab(s)
        ada_next[0] = 24
        mod_finalize(0, 48)

        def ada_more(n):
            for _ in range(n):
                if ada_next[0] < 88:
                    ada_slab(ada_next[0])
                    ada_next[0] += 1

        def modv(k):
            return CS(C_MOD + 16 * k, C_MOD + 16 * k + 16)

        def vec(k):
            return CS(C_VEC + 16 * k, C_VEC + 16 * k + 16)

        def mk_A(dst, sc_k, n_k):
            da, dd = CS(dst, dst + 16)
            sa_, sd_ = modv(sc_k)
            na, nd = vec(n_k)
            P.op('dve', lambda e: e.scalar_tensor_tensor(out=da, in0=sa_, scalar=1.0, in1=na, op0=ALU.add, op1=ALU.mult),
                 reads=[sd_, nd], writes=[dd])

        def mk_G(dst, g_k):
            da, dd = CS(dst, dst + 16)
            ga, gd = modv(g_k)
            P.op('dve', lambda e: e.tensor_scalar(out=da, in0=ga, scalar1=0.5, scalar2=None, op0=ALU.mult),
                 reads=[gd], writes=[dd])
        mk_A(C_A1, 1, 0); mk_G(C_G1, 2)

        def mod_rest():
            ada_more(88)
            mod_finalize(48, 176)
            mk_A(C_A2, 4, 1)
            mk_A(C_A3, 7, 2); mk_G(C_G3, 8)
            mk_A(C_AF, 10, 3)
        C_B1 = C_MOD + 0
        C_B2 = C_MOD + 48
        C_G2 = C_MOD + 80
        C_B3 = C_MOD + 96
        C_BF = C_MOD + 144

        tmp_ctr = [0]

        def tmp():
            k = tmp_ctr[0] % 6
            tmp_ctr[0] += 1
            return TMP(k * 512, k * 512 + 512)

        ab_ctr = {'A': 0, 'B': 0, 'O': 0}

        def ring_bank(kind):
            base = {'A': 0, 'B': 2, 'O': 4}[kind]
            k = ab_ctr[kind] % 2
            ab_ctr[kind] += 1
            return base + k

        def Xc(c):
            return X(c * 512, c * 512 + 512)

        def Hc(c, a=0, n=512):
            return H(c * 512 + a, c * 512 + a + n)

        onesD = CB(B_OD, B_OD + 128)
        onesC = CB(B_OC, B_OC + 128)
        ident = CB(B_ID, B_ID + 128)

        def rms_norm(a_off, b_off, final=False):
            sa, sd = bank(6)
            for c in range(16):
                xa, xd = Xc(c)
                ha, hd = Hc(c)
                P.op('act', lambda e, xa=xa, ha=ha: e.activation(out=ha, in_=xa, func=AF.Square), reads=[xd], writes=[hd])
                P.op('pe', lambda e, ha=ha, c=c: e.matmul(sa, onesD[0], ha, start=(c == 0), stop=(c == 15)),
                     reads=[hd, onesD[1]], writes=[sd])
            sda, sdd = ST(512, 1024)
            P.op('act', lambda e: e.activation(out=sda, in_=sa, func=AF.Sqrt, bias=epsA, scale=1.0), reads=[sd, epsD], writes=[sdd])
            ra, rd = ST(0, 512)
            P.op('dve', lambda e: e.reciprocal(out=ra, in_=sda), reads=[sdd], writes=[rd])
            for c in range(16):
                xa, xd = Xc(c)
                ta, td = tmp()
                P.op('dve', lambda e, xa=xa, ta=ta: e.tensor_tensor(out=ta, in0=xa, in1=ra, op=ALU.mult), reads=[xd, rd], writes=[td])
                Aa, Ad = CS(a_off + c, a_off + c + 1)
                Ba, Bd = CS(b_off + c, b_off + c + 1)
                if final:
                    oa, od = scr_f(c * 2048, 512)
                else:
                    oa, od = Hc(c)
                P.op('act', lambda e, oa=oa, ta=ta, Aa=Aa, Ba=Ba: e.activation(out=oa, in_=ta, func=AF.Identity, scale=Aa, bias=Ba),
                     reads=[td, Ad, Bd], writes=[od])

        def actc(j):
            return scr_bf(j * 1024, 512)

        def ffn(k, g_off, hook=None):
            for s in range(22):
                soA = slab8()
                load_w("w13_%d" % k, 2 * s, 4096, soA)
                soB = slab8()
                load_w("w13_%d" % k, 2 * s + 1, 4096, soB)
                for sub in range(2):
                    j = 2 * s + sub
                    bA = ring_bank('A')
                    bB = ring_bank('B')
                    pA, dA = bank(bA)
                    pB, dB = bank(bB)
                    for (so, pp, dd) in ((soA, pA, dA), (soB, pB, dB)):
                        for kc in range(16):
                            la, ld = SL(so + kc * 256 + sub * 128, so + kc * 256 + sub * 128 + 128)
                            ha, hd = Hc(kc)
                            P.op('pe', lambda e, pp=pp, la=la, ha=ha, kc=kc: e.matmul(pp, la, ha, start=(kc == 0), stop=(kc == 15)),
                                 reads=[ld, hd], writes=[dd])
                    ta, td = tmp()
                    P.op('act', lambda e, ta=ta, pA=pA: e.activation(out=ta, in_=pA, func=AF.Silu), reads=[dA], writes=[td])
                    aa, ad = actc(j)
                    P.op('dve', lambda e, aa=aa, ta=ta, pB=pB: e.tensor_tensor(out=aa, in0=ta, in1=pB, op=ALU.mult),
                         reads=[td, dB], writes=[ad])
                if hook:
                    hook(2)
            for i in range(16):
                if hook:
                    hook(2)
                so = slab11()
                load_w("w2_%d" % k, i, 5632, so)
                bO = ring_bank('O')
                pO, dO = bank(bO)
                for j in range(NFF):
                    la, ld = SL(so + j * 128, so + j * 128 + 128)
                    aa, ad = actc(j)
                    P.op('pe', lambda e, pO=pO, la=la, aa=aa, j=j: e.matmul(pO, la, aa, start=(j == 0), stop=(j == NFF - 1)),
                         reads=[ld, ad], writes=[dO])
                xa, xd = Xc(i)
                ga, gd = CS(g_off + i, g_off + i + 1)
                P.op('dve', lambda e, xa=xa, pO=pO, ga=ga: e.scalar_tensor_tensor(out=xa, in0=pO, scalar=ga, in1=xa, op0=ALU.mult, op1=ALU.add),
                     reads=[dO, gd, xd], writes=[xd])

        def proj_fm(so, sub, pp, dd, nk=16, rhs_fn=None):
            for kc in range(nk):
                la, ld = SL(so + kc * 256 + sub * 128, so + kc * 256 + sub * 128 + 128)
                ha, hd = rhs_fn(kc) if rhs_fn else Hc(kc)
                P.op('pe', lambda e, pp=pp, la=la, ha=ha, kc=kc, nk=nk: e.matmul(pp, la, ha, start=(kc == 0), stop=(kc == nk - 1)),
                     reads=[ld, hd], writes=[dd])

        def load_slab(nm, idx):
            so = slab8()
            load_w(nm, idx, 4096, so)
            return so

        oTv_dbg = outT.rearrange("(c p) t -> p c t", p=128)

        def dump(byte0, nchunks, colbase):
            for c in range(nchunks):
                a_, d_ = scr_bf(byte0 + c * 1024, 512)
                dma('pool', oTv_dbg[:, c, colbase:colbase + 512], a_, [d_], [])

        GAM = [1.0 - 2.0 ** (-5.0 - h) for h in range(4)]
        sm_ctr = [0]

        def mixer(t, mstop=None):
            t0 = t * T
            a_, d_ = TAB(0, 512)
            dma('sp', a_, cosT[:, t0:t0 + T], [], [d_])
            a_, d_ = TAB(512, 1024)
            dma('sp', a_, sinT[:, t0:t0 + T], [], [d_])
            cosA, cosD = TAB(0, 512)
            sinA, sinD = TAB(512, 1024)
            s1a, s1d = bank(4)
            s2a, s2d = bank(5)
            for s in range(4):
                soA = load_slab('win', s)
                soB = load_slab('win', 4 + s)
                for sub in range(2):
                    cc = 2 * s + sub
                    pA, dA = bank(ring_bank('A'))
                    pB, dB = bank(ring_bank('B'))
                    proj_fm(soA, sub, pA, dA)
                    proj_fm(soB, sub, pB, dB)
                    ta, td = tmp()
                    P.op('act', lambda e, ta=ta, pB=pB: e.activation(out=ta, in_=pB, func=AF.Sigmoid), reads=[dB], writes=[td])
                    yo = (cc % 2) * 544
                    ha_, hd_ = CS(C_HALO + cc * 30, C_HALO + cc * 30 + 30)
                    y0a, y0d = YB(yo, yo + 30)
                    P.op('dve', lambda e, y0a=y0a, ha_=ha_: e.tensor_copy(out=y0a, in_=ha_), reads=[hd_], writes=[y0d])
                    y1a, y1d = YB(yo + 30, yo + 542)
                    P.op('dve', lambda e, y1a=y1a, pA=pA, ta=ta: e.tensor_tensor(out=y1a, in0=pA, in1=ta, op=ALU.mult),
                         reads=[dA, td], writes=[y1d], small=True)
                    yta, ytd = YB(yo + 512, yo + 542)
                    P.op('dve', lambda e, ha_=ha_, yta=yta: e.tensor_copy(out=ha_, in_=yta), reads=[ytd], writes=[hd_])
                    ca_, cd_ = scr_f(cc * 2048, 512)
                    for j in range(31):
                        ya, yd = YB(yo + j, yo + j + 512)
                        wa, wd = CS(C_CW + cc * 31 + j, C_CW + cc * 31 + j + 1)
                        if j == 0:
                            ba_, bd_ = CS(C_CV + cc, C_CV + cc + 1)
                            P.op('dve', lambda e, ca_=ca_, ya=ya, wa=wa, ba_=ba_: e.tensor_scalar(out=ca_, in0=ya, scalar1=wa, scalar2=ba_, op0=ALU.mult, op1=ALU.add),
                                 reads=[yd, wd, bd_], writes=[cd_])
                        else:
                            P.op('dve', lambda e, ca_=ca_, ya=ya, wa=wa: e.scalar_tensor_tensor(out=ca_, in0=ya, scalar=wa, in1=ca_, op0=ALU.mult, op1=ALU.add),
                                 reads=[yd, wd, cd_], writes=[cd_])
                    cba_, cbd_ = scr_bf(49152 + (cc % 2) * 2048, 512)
                    csa_, csd_ = scr_bf(49152 + (cc % 2) * 2048 + 1024, 512)
                    P.op('act', lambda e, cba_=cba_, ca_=ca_: e.activation(out=cba_, in_=ca_, func=AF.Identity), reads=[cd_], writes=[cbd_])
                    P.op('act', lambda e, csa_=csa_, ca_=ca_: e.activation(out=csa_, in_=ca_, func=AF.Square), reads=[cd_], writes=[csd_])
                    P.op('pe', lambda e, cba_=cba_, cc=cc: e.matmul(s1a, onesC[0], cba_, start=(cc == 0), stop=(cc == 7)),
                         reads=[cbd_, onesC[1]], writes=[s1d])
                    P.op('pe', lambda e, csa_=csa_, cc=cc: e.matmul(s2a, onesC[0], csa_, start=(cc == 0), stop=(cc == 7)),
                         reads=[csd_, onesC[1]], writes=[s2d])
            msa, msd = tmp()
            P.op('act', lambda e: e.activation(out=msa, in_=s1a, func=AF.Square), reads=[s1d], writes=[msd])
            vra, vrd = tmp()
            P.op('dve', lambda e: e.tensor_tensor(out=vra, in0=s2a, in1=msa, op=ALU.subtract), reads=[s2d, msd], writes=[vrd])
            sda, sdd = ST(512, 1024)
            P.op('act', lambda e: e.activation(out=sda, in_=vra, func=AF.Sqrt, bias=epsA, scale=1.0), reads=[vrd, epsD], writes=[sdd])
            ra, rd = ST(0, 512)
            P.op('dve', lambda e: e.reciprocal(out=ra, in_=sda), reads=[sdd], writes=[rd])
            nma, nmd = ST(1024, 1536)
            P.op('dve', lambda e: e.scalar_tensor_tensor(out=nma, in0=s1a, scalar=-1.0, in1=ra, op0=ALU.mult, op1=ALU.mult),
                 reads=[s1d, rd], writes=[nmd])
            for cc in range(8):
                ca_, cd_ = scr_f(cc * 2048, 512)
                t1a, t1d = tmp()
                P.op('dve', lambda e, t1a=t1a, ca_=ca_: e.tensor_tensor(out=t1a, in0=ca_, in1=ra, op=ALU.mult), reads=[cd_, rd], writes=[t1d])
                t2a, t2d = tmp()
                P.op('dve', lambda e, t2a=t2a, t1a=t1a: e.tensor_tensor(out=t2a, in0=t1a, in1=nma, op=ALU.add), reads=[t1d, nmd], writes=[t2d])
                ga, gd = CS(C_CV + 8 + cc, C_CV + 8 + cc + 1)
                ba_, bd_ = CS(C_CV + 16 + cc, C_CV + 16 + cc + 1)
                za, zd = scr_bf(16384 + cc * 1024, 512)
                P.op('act', lambda e, za=za, t2a=t2a, ga=ga, ba_=ba_: e.activation(out=za, in_=t2a, func=AF.Silu, scale=ga, bias=ba_),
                     reads=[t2d, gd, bd_], writes=[zd])

            if DBG_DUMPS == 1:
                dump(16384, 8, 512)
            if mstop == 'mA':
                return
            def qT(c, a=0, n=512):
                return scr_bf(24576 + c * 1024 + a * 2, n)

            def kT(c, a=0, n=512):
                return scr_bf(32768 + c * 1024 + a * 2, n)
            for which, base_slab, dst in (('q', 8, qT), ('k', 12, kT)):
                for h in range(4):
                    so = load_slab('win', base_slab + h)
                    p1, d1 = bank(ring_bank('A'))
                    p2, d2 = bank(ring_bank('B'))
                    proj_fm(so, 0, p1, d1)
                    proj_fm(so, 1, p2, d2)
                    t1a, t1d = tmp()
                    P.op('dve', lambda e, t1a=t1a, p1=p1: e.tensor_tensor(out=t1a, in0=p1, in1=cosA, op=ALU.mult), reads=[d1, cosD], writes=[t1d])
                    t2a, t2d = tmp()
                    P.op('dve', lambda e, t2a=t2a, p2=p2: e.tensor_tensor(out=t2a, in0=p2, in1=sinA, op=ALU.mult), reads=[d2, sinD], writes=[t2d])
                    oa, od = dst(2 * h)
                    P.op('dve', lambda e, oa=oa, t1a=t1a, t2a=t2a: e.tensor_tensor(out=oa, in0=t1a, in1=t2a, op=ALU.subtract), reads=[t1d, t2d], writes=[od])
                    t3a, t3d = tmp()
                    P.op('dve', lambda e, t3a=t3a, p1=p1: e.tensor_tensor(out=t3a, in0=p1, in1=sinA, op=ALU.mult), reads=[d1, sinD], writes=[t3d])
                    t4a, t4d = tmp()
                    P.op('dve', lambda e, t4a=t4a, p2=p2: e.tensor_tensor(out=t4a, in0=p2, in1=cosA, op=ALU.mult), reads=[d2, cosD], writes=[t4d])
                    oa, od = dst(2 * h + 1)
                    P.op('dve', lambda e, oa=oa, t3a=t3a, t4a=t4a: e.tensor_tensor(out=oa, in0=t3a, in1=t4a, op=ALU.add), reads=[t3d, t4d], writes=[od])

            if DBG_DUMPS == 1:
                dump(24576, 16, 1024)
            if mstop == 'mC1':
                return
            def ktm(blk, a, n):
                return scr_bf(40960 + blk * 2048 + a * 2, n)
            for blk in range(DBG_NBLK):
                kb = (ring_bank('A'), ring_bank('B'))
                for dc in range(8):
                    ia, idd = kT(dc, blk * 128, 128)
                    pa_, pd_ = bank(kb[dc // 4], (dc % 4) * 128, 128)
                    P.op('pe', lambda e, pa_=pa_, ia=ia: e.matmul(pa_, ia, ident[0], start=True, stop=True), reads=[idd, ident[1]], writes=[pd_])
                for h in range(0 if DBG_NOACT else 4):
                    pa_, pd_ = bank(kb[h // 2], (h % 2) * 256, 256)
                    oa, od = ktm(blk, h * 256, 256)
                    ka, kd = CS(C_KDEC + h, C_KDEC + h + 1)
                    P.op('dve', lambda e, oa=oa, pa_=pa_, ka=ka: e.tensor_scalar(out=oa, in0=pa_, scalar1=ka, scalar2=None, op0=ALU.mult), reads=[pd_, kd], writes=[od])

            if mstop == 'mC2':
                return
            def vtm(blk, a, n):
                return scr_bf(blk * 4096 + a * 2, n)
            vr = [0]
            for s in range(DBG_VS):
                so = load_slab('win', 16 + s)
                for blk in range(DBG_VB):
                    r = vr[0] % 8
                    vr[0] += 1
                    pa_, pd_ = bank(r % 4, 0, 256)
                    for kc in range(16):
                        la, ld = Hc(kc, blk * 128, 128)
                        ra_, rd_ = SL(so + kc * 256, so + kc * 256 + 256)
                        P.op('pe', lambda e, pa_=pa_, la=la, ra_=ra_, kc=kc: e.matmul(pa_, la, ra_, start=(kc == 0), stop=(kc == 15)),
                             reads=[ld, rd_], writes=[pd_])
                    oa, od = vtm(blk, s * 256, 256)
                    P.op('act', lambda e, oa=oa, pa_=pa_: e.activation(out=oa, in_=pa_, func=AF.Identity), reads=[pd_], writes=[od])

            if DBG_DUMPS == 1:
                dump(0, 16, 1536)
            if mstop == 'mC3':
                return
            def ontm(blk, a, n):
                return scr_bf(49152 + blk * 4096 + a * 2, n)
            unit = 0
            for blk in range(4):
                for h in range(4):
                    psS, dS = bank(6 + (unit % 2), 0, 128)
                    for dc in range(2):
                        la, ld = kT(2 * h + dc, blk * 128, 128)
                        ra_, rd_ = qT(2 * h + dc, blk * 128, 128)
                        P.op('pe', lambda e, psS=psS, la=la, ra_=ra_, dc=dc: e.matmul(psS, la, ra_, start=(dc == 0), stop=(dc == 1)),
                             reads=[ld, rd_], writes=[dS])
                    sto = B_ST + (unit % 4) * 128
                    sTa, sTd = CB(sto, sto + 128)
                    mka, mkd = CS(C_MASK + h * 128, C_MASK + h * 128 + 128)
                    P.op('dve', lambda e, sTa=sTa, psS=psS, mka=mka: e.tensor_tensor(out=sTa, in0=psS, in1=mka, op=ALU.mult), reads=[dS, mkd], writes=[sTd])
                    pO, dO = bank(ring_bank('O'))
                    va, vd = vtm(blk, h * 512, 512)
                    P.op('pe', lambda e, pO=pO, sTa=sTa, va=va: e.matmul(pO, sTa, va, start=True, stop=False), reads=[sTd, vd], writes=[dO])
                    for dc in range(2):
                        la, ld = qT(2 * h + dc, blk * 128, 128)
                        ra_, rd_ = RBF((h * 2 + dc) * 512, (h * 2 + dc) * 512 + 512)
                        P.op('pe', lambda e, pO=pO, la=la, ra_=ra_, dc=dc: e.matmul(pO, la, ra_, start=False, stop=(dc == 1)),
                             reads=[ld, rd_], writes=[dO])
                    g128 = float(GAM[h] ** 128)
                    for dc in range(2):
                        pR, dR = bank((unit % 2) * 2 + dc)
                        la, ld = ktm(blk, h * 256 + dc * 128, 128)
                        P.op('pe', lambda e, pR=pR, la=la, va=va: e.matmul(pR, la, va, start=True, stop=True), reads=[ld, vd], writes=[dR])
                        r32a, r32d = R32((h * 2 + dc) * 512, (h * 2 + dc) * 512 + 512)
                        P.op('dve', lambda e, r32a=r32a, pR=pR, g128=g128: e.scalar_tensor_tensor(out=r32a, in0=r32a, scalar=g128, in1=pR, op0=ALU.mult, op1=ALU.add),
                             reads=[r32d, dR], writes=[r32d])
                        rba, rbd = RBF((h * 2 + dc) * 512, (h * 2 + dc) * 512 + 512)
                        P.op('act', lambda e, rba=rba, r32a=r32a: e.activation(out=rba, in_=r32a, func=AF.Identity, scale=0.0625), reads=[r32d], writes=[rbd])
                    sm = C_SM + (sm_ctr[0] % 4) * 16
                    sm_ctr[0] += 1
                    st6a, st6d = CS(sm, sm + 6)
                    mva, mvd = CS(sm + 6, sm + 8)
                    P.op('dve', lambda e, st6a=st6a, pO=pO: e.bn_stats(out=st6a, in_=pO), reads=[dO], writes=[st6d], small=True)
                    P.op('dve', lambda e, mva=mva, st6a=st6a: e.bn_aggr(out=mva, in_=st6a), reads=[st6d], writes=[mvd], small=True)
                    vea, ved = CS(sm + 8, sm + 9)
                    epa, epd = CS(C_EPSP + h, C_EPSP + h + 1)
                    vara, vard = CS(sm + 7, sm + 8)
                    mea, med = CS(sm + 6, sm + 7)
                    P.op('dve', lambda e, vea=vea, vara=vara, epa=epa: e.tensor_tensor(out=vea, in0=vara, in1=epa, op=ALU.add), reads=[mvd, epd], writes=[ved], small=True)
                    sqa, sqd = CS(sm + 9, sm + 10)
                    P.op('act', lambda e, sqa=sqa, vea=vea: e.activation(out=sqa, in_=vea, func=AF.Sqrt), reads=[ved], writes=[sqd], small=True)
                    rsa, rsd = CS(sm + 10, sm + 11)
                    P.op('dve', lambda e, rsa=rsa, sqa=sqa: e.reciprocal(out=rsa, in_=sqa), reads=[sqd], writes=[rsd], small=True)
                    nra, nrd = CS(sm + 11, sm + 12)
                    P.op('dve', lambda e, nra=nra, mea=mea, rsa=rsa: e.scalar_tensor_tensor(out=nra, in0=mea, scalar=-1.0, in1=rsa, op0=ALU.mult, op1=ALU.mult),
                         reads=[mvd, rsd], writes=[nrd], small=True)
                    ona, ond = ontm(blk, h * 512, 512)
                    P.op('dve', lambda e, ona=ona, pO=pO, rsa=rsa, nra=nra: e.tensor_scalar(out=ona, in0=pO, scalar1=rsa, scalar2=nra, op0=ALU.mult, op1=ALU.add),
                         reads=[dO, rsd, nrd], writes=[ond])
                    unit += 1

            if DBG_DUMPS == 2:
                dump(49152, 16, 512)
            if mstop == 'mC5':
                return
            def sgT(i):
                return scr_bf(i * 1024, 512)
            for ip in range(8):
                so = load_slab('win', 24 + ip)
                for sub in range(2):
                    i = 2 * ip + sub
                    pp, dd = bank(ring_bank('A' if sub == 0 else 'B'))
                    proj_fm(so, sub, pp, dd)
                    oa, od = sgT(i)
                    P.op('act', lambda e, oa=oa, pp=pp: e.activation(out=oa, in_=pp, func=AF.Silu), reads=[dd], writes=[od])

            if mstop == 'mC4':
                return
            def oT(i):
                return scr_bf(24576 + i * 1024, 512)
            for ec in range(16):
                tb = ring_bank('A' if ec % 2 == 0 else 'B')
                for blk in range(4):
                    ia, idd = ontm(blk, ec * 128, 128)
                    pa_, pd_ = bank(tb, blk * 128, 128)
                    P.op('pe', lambda e, pa_=pa_, ia=ia: e.matmul(pa_, ia, ident[0], start=True, stop=True), reads=[idd, ident[1]], writes=[pd_])
                pa_, pd_ = bank(tb)
                ta, td = tmp()
                ga, gd = CS(C_VEC + 64 + ec, C_VEC + 64 + ec + 1)
                ba_, bd_ = CS(C_VEC + 80 + ec, C_VEC + 80 + ec + 1)
                P.op('dve', lambda e, ta=ta, pa_=pa_, ga=ga, ba_=ba_: e.tensor_scalar(out=ta, in0=pa_, scalar1=ga, scalar2=ba_, op0=ALU.mult, op1=ALU.add),
                     reads=[pd_, gd, bd_], writes=[td])
                sa_, sd_ = sgT(ec)
                oa, od = oT(ec)
                P.op('dve', lambda e, oa=oa, ta=ta, sa_=sa_: e.tensor_tensor(out=oa, in0=ta, in1=sa_, op=ALU.mult), reads=[td, sd_], writes=[od])

            if DBG_DUMPS == 2:
                dump(24576, 16, 1024)
            if mstop == 'mC6':
                return
            def zc(kc):
                return scr_bf(16384 + kc * 1024, 512)

            def mg(i):
                return scr_bf(i * 1024, 512)
            for ip in range(8):
                so = load_slab('win', 32 + ip)
                sg = []
                for sub in range(2):
                    pp, dd = bank(ring_bank('A'))
                    proj_fm(so, sub, pp, dd)
                    ta, td = TMP(sub * 512, sub * 512 + 512)
                    P.op('act', lambda e, ta=ta, pp=pp: e.activation(out=ta, in_=pp, func=AF.Sigmoid), reads=[dd], writes=[td])
                    sg.append((ta, td))
                so = slab8()
                load_w('pw', ip, 2048, so)
                for sub in range(2):
                    pp, dd = bank(ring_bank('B'))
                    proj_fm(so, sub, pp, dd, nk=8, rhs_fn=zc)
                    ta, td = sg[sub]
                    P.op('dve', lambda e, ta=ta, pp=pp: e.tensor_tensor(out=ta, in0=ta, in1=pp, op=ALU.mult), reads=[td, dd], writes=[td])
                so = load_slab('win', 40 + ip)
                sr = []
                for sub in range(2):
                    pp, dd = bank(ring_bank('A'))
                    proj_fm(so, sub, pp, dd)
                    ta, td = TMP((2 + sub) * 512, (2 + sub) * 512 + 512)
                    P.op('act', lambda e, ta=ta, pp=pp: e.activation(out=ta, in_=pp, func=AF.Sigmoid), reads=[dd], writes=[td])
                    sr.append((ta, td))
                so = load_slab('wo', ip)
                for sub in range(2):
                    i = 2 * ip + sub
                    pp, dd = bank(ring_bank('B'))
                    proj_fm(so, sub, pp, dd, nk=16, rhs_fn=oT)
                    ta, td = sr[sub]
                    P.op('dve', lambda e, ta=ta, pp=pp: e.tensor_tensor(out=ta, in0=ta, in1=pp, op=ALU.mult), reads=[td, dd], writes=[td])
                    ca_, cd_ = sg[sub]
                    oa, od = mg(i)
                    P.op('dve', lambda e, oa=oa, ta=ta, ca_=ca_: e.tensor_tensor(out=oa, in0=ta, in1=ca_, op=ALU.add), reads=[td, cd_], writes=[od])
            tmp_ctr[0] = 4
            if DBG_DUMPS == 2:
                dump(0, 16, 1536)
            if mstop == 'mD':
                return
            for ip in range(8):
                so = load_slab('wout', ip)
                for sub in range(2):
                    i2 = 2 * ip + sub
                    pO, dO = bank(ring_bank('O'))
                    proj_fm(so, sub, pO, dO, nk=16, rhs_fn=mg)
                    xa, xd = Xc(i2)
                    ga, gd = CS(C_G2 + i2, C_G2 + i2 + 1)
                    P.op('dve', lambda e, xa=xa, pO=pO, ga=ga: e.scalar_tensor_tensor(out=xa, in0=pO, scalar=ga, in1=xa, op0=ALU.mult, op1=ALU.add),
                         reads=[dO, gd, xd], writes=[xd])

        X3 = X.t[:, :].rearrange("p (c t) -> p c t", t=512)
        xTv = xT.rearrange("(c p) t -> p c t", p=128)
        oTv = outT.rearrange("(c p) t -> p c t", p=128)
        for t in range(n_tiles):
            t0 = t * T
            cur_tile[0] = t
            dma('sp', X3, xTv[:, :, t0:t0 + T], [], [('X', 0, 32768)])
            done = False
            rms_norm(C_A1, C_B1)
            ffn(0, C_G1, hook=(ada_more if t == 0 else None))
            if t == 0:
                mod_rest()
            if stop != 'ffn1':
                rms_norm(C_A2, C_B2)
                mixer(t, stop)
                if stop == 'mix' or (stop is not None and stop.startswith('m')):
                    pass
                else:
                    rms_norm(C_A3, C_B3)
                    ffn(1, C_G3)
                    if stop != 'ffn2':
                        rms_norm(C_AF, C_BF, final=True)
                        O3 = SCRt[:, 0:8192].rearrange("p (c t) -> p c t", t=512)
                        dma('sp', oTv[:, :, t0:t0 + T], O3, [('SCR', 0, 32768)], [])
                        done = True
            if not done:
                dma('sp', oTv[:, :, t0:t0 + T], X3, [('X', 0, 32768)], [])

        P.emit(nc, block, sems, rings)
    nc._prog = P
    return nc


def _slabs(w, F):
    K, N = w.shape
    a = w.reshape(K // 128, 128, N // F, F).transpose(2, 1, 0, 3)
    return np.ascontiguousarray(a).reshape(N // F, 128, (K // 128) * F)


def _w2_slabs(w):
    a = w.reshape(NFF, 128, 16, 128).transpose(2, 1, 0, 3)
    return np.ascontiguousarray(a).reshape(16, 128, NFF * 128)


def _pv(v):
    return np.ascontiguousarray(v.reshape(-1, 128).T)


def _prep_shared(inp):
    sh = {}
    f = lambda k: np.asarray(inp[k], dtype=np.float32)
    sh["adaw"] = _slabs(np.concatenate([f("ada_w")[0], f("ada_f_w")], axis=1), 256)
    sh["adab"] = np.concatenate([_pv(f("ada_b")[0]), _pv(f("ada_f_b"))], axis=1)
    sh["vecs"] = np.concatenate([_pv(f("ffn1_norm")[0]), _pv(f("mix_norm")[0]), _pv(f("ffn2_norm")[0]),
                                 _pv(f("final_norm")), _pv(f("ret_gn_g")[0]), _pv(f("ret_gn_b")[0])], axis=1)
    cw = f("conv_dw_w")[0]
    sh["convw"] = np.ascontiguousarray(cw.reshape(31, 8, 128).transpose(2, 1, 0)).reshape(128, 248)
    sh["convv"] = np.concatenate([_pv(f("conv_dw_b")[0]), _pv(f("conv_ln_g")[0]), _pv(f("conv_ln_b")[0])], axis=1)
    for k, (a, b, c) in enumerate((("ffn1_w1", "ffn1_w3", "ffn1_w2"), ("ffn2_w1", "ffn2_w3", "ffn2_w2"))):
        s1 = _slabs(f(a)[0], 256)
        s3 = _slabs(f(b)[0], 256)
        sh["w13_%d" % k] = np.ascontiguousarray(np.stack([s1, s3], axis=1)).reshape(44, 128, 4096)
        sh["w2_%d" % k] = _w2_slabs(f(c)[0])
    sh["win"] = _slabs(f("w_in")[0], 256)
    sh["pw"] = _slabs(f("conv_pw_w")[0], 256)
    sh["wo"] = _slabs(f("ret_w_o")[0], 256)
    sh["wout"] = _slabs(f("w_out")[0], 256)
    pos = np.arange(S, dtype=np.float32)
    inv_freq = (np.float32(10000.0) ** (-np.arange(0, 256, 2, dtype=np.float32) / np.float32(256))).astype(np.float32)
    ang = (pos[:, None] * inv_freq[None, :]).astype(np.float32)
    sh["cosT"] = np.ascontiguousarray(np.cos(ang).T.astype(np.float32))
    sh["sinT"] = np.ascontiguousarray(np.sin(ang).T.astype(np.float32))
    cstv = np.zeros((128, 520), dtype=np.float32)
    pidx = np.arange(128, dtype=np.float64)
    for h in range(4):
        gam = 1.0 - 2.0 ** (-5.0 - h)
        m = pidx[:, None]
        c = pidx[None, :]
        same = (m // 64) == (c // 64)
        causal_x = ((m // 64) == 0) & ((c // 64) == 1)
        wgt = np.where(same, gam ** np.abs(c - m), np.where(causal_x, gam ** (c - m), 0.0))
        s = gam ** (c + 1.0)
        cstv[:, h * 128:(h + 1) * 128] = (wgt / s / 16.0).astype(np.float32)
        cstv[:, 512 + h] = (gam ** (127.0 - pidx)).astype(np.float32)
        cstv[:, 516 + h] = (EPS / (gam ** (2.0 * (pidx + 1.0)))).astype(np.float32)
    sh["cst"] = cstv
    sh["identd"] = np.eye(128, dtype=np.float32)
    return sh


_NC_CACHE = {}


def kernel(**inputs):
    x = np.asarray(inputs["x"], dtype=np.float32)
    c = np.asarray(inputs["c"], dtype=np.float32)
    shared = _prep_shared(inputs)
    key = (NT if DBG_TILES is None else DBG_TILES, STOP)
    if key not in _NC_CACHE:
        _NC_CACHE[key] = build_program(n_tiles=key[0], stop=key[1])
    nc = _NC_CACHE[key]
    in_maps = []
    for b in range(N_CORES):
        m = dict(shared)
        m["xT"] = np.ascontiguousarray(x[b].T)
        m["cvec"] = _pv(c[b])
        in_maps.append(m)
    res = run_bass_kernel_spmd(nc, in_maps, core_ids=list(range(N_CORES)))
    out = np.empty((N_CORES, S, D), dtype=np.float32)
    for b in range(N_CORES):
        out[b] = res.results[b]["outT"].T
    return out
```

```python
import numpy as np
import concourse.bass as bass
import concourse.mybir as mybir
from concourse.bass_utils import run_bass_kernel_spmd

F32 = mybir.dt.float32
BF16 = mybir.dt.bfloat16
AF = mybir.ActivationFunctionType
ALU = mybir.AluOpType

D = 2048
S = 2048
T = 512
NT = S // T
DFF = 5632
NFF = DFF // 128
EPS = 1e-6
N_CORES = 8
STOP = None
DBG_TILES = None
DBG_NBLK = 4
DBG_NOACT = False
DBG_VS = 8
DBG_DUMPS = 0
DBG_VB = 4


class Prog:
    GR = 256

    def __init__(self):
        self.ops = []
        self.lw = {}
        self.rd = {}

    def _gr(self, reg):
        sp, lo, hi = reg
        gr = 2048 if sp == 'PS' else self.GR
        return [(sp, g) for g in range(lo // gr, (hi - 1) // gr + 1)]

    def op(self, eng, fn, reads=(), writes=(), dma=False, small=False):
        idx = len(self.ops)
        deps = {}

        def add(d):
            if d is None or d == idx:
                return
            o = self.ops[d]
            key = ('dma', d) if o['dma'] else o['eng']
            if deps.get(key, -1) < d:
                deps[key] = d

        rg = [g for r in reads for g in self._gr(r)]
        wg = [g for w in writes for g in self._gr(w)]
        for g in rg:
            add(self.lw.get(g))
        for g in wg:
            add(self.lw.get(g))
            for d in self.rd.get(g, {}).values():
                add(d)
        key = ('dma', idx) if dma else eng
        for g in rg:
            self.rd.setdefault(g, {})[key] = idx
        for g in wg:
            self.lw[g] = idx
            self.rd[g] = {}
        self.ops.append(dict(eng=eng, fn=fn, deps=deps, dma=dma, small=small))
        return idx

    def emit(self, nc, block, sems, dma_rings):
        ops = self.ops
        NS = len(next(iter(dma_rings.values())))
        dcount = {q: 0 for q in dma_rings}
        for o in ops:
            if o['dma']:
                q = o['eng']
                n = dcount[q]
                dcount[q] += 1
                o['dsem'] = dma_rings[q][n % NS]
                o['dval'] = 16 * (n // NS + 1)
        is_ms = [False] * len(ops)
        for o in ops:
            for key, d in o['deps'].items():
                if isinstance(key, tuple):
                    continue
                if (not o['dma']) and key == o['eng'] and not ops[d]['small']:
                    continue
                is_ms[d] = True
        mcount = {}
        for i, o in enumerate(ops):
            if is_ms[i]:
                mcount[o['eng']] = mcount.get(o['eng'], 0) + 1
                o['ms'] = mcount[o['eng']]
        by_eng = {}
        for i, o in enumerate(ops):
            by_eng.setdefault(o['eng'], []).append(i)
        final_waits = {q: [] for q in dma_rings}
        for q in dma_rings:
            last = {}
            for o in ops:
                if o['dma'] and o['eng'] == q:
                    last[id(o['dsem'])] = (o['dsem'], o['dval'])
            final_waits[q] = list(last.values())

        def run_engine(name, e):
            waited = {}

            def wait(sem, val):
                k = id(sem)
                if waited.get(k, 0) >= val:
                    return
                waited[k] = val
                e.wait_ge(sem, val)

            for i in by_eng.get(name, []):
                o = ops[i]
                for key, d in o['deps'].items():
                    od = ops[d]
                    if isinstance(key, tuple):
                        wait(od['dsem'], od['dval'])
                    else:
                        if (not o['dma']) and key == name and not od['small']:
                            continue
                        wait(sems[key], od['ms'])
                if o['dma'] and o['dval'] > 16:
                    wait(o['dsem'], o['dval'] - 16)
                o['dbg_waits'] = [(k, ops[d].get('ms', ops[d].get('dval')), d) for k, d in o['deps'].items()
                                  if isinstance(k, tuple) or o['dma'] or k != name or ops[d]['small']]
                ins = o['fn'](e)
                if o['dma']:
                    ins.then_inc(o['dsem'], 16)
                elif is_ms[i]:
                    ins.then_inc(sems[name], 1)
            if name in final_waits:
                for sem, val in final_waits[name]:
                    wait(sem, val)

        @block.tensor
        def _(e):
            run_engine('pe', e)

        @block.scalar
        def _(e):
            run_engine('act', e)

        @block.vector
        def _(e):
            run_engine('dve', e)

        @block.gpsimd
        def _(e):
            run_engine('pool', e)

        @block.sync
        def _(e):
            run_engine('sp', e)


class Buf:
    def __init__(self, name, t, el):
        self.name, self.t, self.el = name, t, el

    def __call__(self, a, b, p0=0, p1=128):
        return self.t[p0:p1, a:b], (self.name, a * self.el, b * self.el)


def build_program(n_tiles=NT, stop=None):
    nc = bass.Bass("TRN2", target_bir_lowering=False)
    dt = nc.dram_tensor
    xT = dt("xT", [D, S], F32, kind="ExternalInput").ap()
    cvec = dt("cvec", [128, 16], F32, kind="ExternalInput").ap()
    adab = dt("adab", [128, 176], F32, kind="ExternalInput").ap()
    vecs = dt("vecs", [128, 96], F32, kind="ExternalInput").ap()
    convw = dt("convw", [128, 248], F32, kind="ExternalInput").ap()
    convv = dt("convv", [128, 24], F32, kind="ExternalInput").ap()
    cst = dt("cst", [128, 520], F32, kind="ExternalInput").ap()
    identd = dt("identd", [128, 128], F32, kind="ExternalInput").ap()
    cosT = dt("cosT", [128, S], F32, kind="ExternalInput").ap()
    sinT = dt("sinT", [128, S], F32, kind="ExternalInput").ap()
    adaw = dt("adaw", [88, 128, 4096], F32, kind="ExternalInput").ap()
    w13 = [dt("w13_%d" % k, [44, 128, 4096], F32, kind="ExternalInput").ap() for k in range(2)]
    w2 = [dt("w2_%d" % k, [16, 128, 5632], F32, kind="ExternalInput").ap() for k in range(2)]
    win = dt("win", [48, 128, 4096], F32, kind="ExternalInput").ap()
    pw = dt("pw", [8, 128, 2048], F32, kind="ExternalInput").ap()
    wo = dt("wo", [8, 128, 4096], F32, kind="ExternalInput").ap()
    wout = dt("wout", [8, 128, 4096], F32, kind="ExternalInput").ap()
    outT = dt("outT", [D, S], F32, kind="ExternalOutput").ap()

    SLAB_BYTES = 33792
    NCS = 1700
    from contextlib import ExitStack
    with ExitStack() as es:
        def sb(name, n, dtype):
            return es.enter_context(nc.sbuf_tensor(name, [128, n], dtype))
        X = Buf('X', sb("X", 8192, F32), 4)
        H = Buf('H', sb("H", 8192, BF16), 2)
        R32 = Buf('R32', sb("R32", 4096, F32), 4)
        RBF = Buf('RBF', sb("RBF", 4096, BF16), 2)
        TAB = Buf('TAB', sb("TAB", 1024, F32), 4)
        SL = Buf('SL', sb("SL", SLAB_BYTES // 2, BF16), 2)
        SCRt = sb("SCR", 16384, F32)
        SCRF = Buf('SCR', SCRt, 4)
        YB = Buf('YB', sb("YB", 2 * 544, F32), 4)
        TMP = Buf('TMP', sb("TMP", 6 * 512, F32), 4)
        ST = Buf('ST', sb("ST", 3 * 512, F32), 4)
        CS = Buf('CS', sb("CS", NCS, F32), 4)
        CB = Buf('CB', sb("CB", 1024, BF16), 2)
        PSt = es.enter_context(nc.psum_tensor("PS", [128, 8 * 512], F32))
        PS = Buf('PS', PSt, 4)
        sems = {k: es.enter_context(nc.semaphore("s_" + k)) for k in ('pe', 'act', 'dve', 'pool', 'sp')}
        NSR = 6
        rings = {q: [es.enter_context(nc.semaphore("d_%s%d" % (q, i))) for i in range(NSR)] for q in ('pool', 'sp')}
        block = es.enter_context(nc.Block())

        P = Prog()

        def scr_bf(byte0, n):
            ap = SCRt[:, byte0 // 4:(byte0 + 2 * n) // 4].bitcast(BF16)
            return ap, ('SCR', byte0, byte0 + 2 * n)

        def scr_f(byte0, n):
            return SCRF(byte0 // 4, byte0 // 4 + n)

        def bank(b, a=0, n=512):
            return PS(b * 512 + a, b * 512 + a + n)

        o = [0]

        def cs_alloc(n):
            a = o[0]
            o[0] += n
            return a
        C_C = cs_alloc(16)
        C_ADAB = cs_alloc(176)
        C_MOD = cs_alloc(176)
        C_A1 = cs_alloc(16); C_G1 = cs_alloc(16); C_A2 = cs_alloc(16)
        C_A3 = cs_alloc(16); C_G3 = cs_alloc(16); C_AF = cs_alloc(16)
        C_VEC = cs_alloc(96)
        C_CW = cs_alloc(248)
        C_CV = cs_alloc(24)
        C_CST = cs_alloc(520)
        C_HALO = cs_alloc(240)
        C_EPS = cs_alloc(1)
        C_SM = cs_alloc(4 * 16)
        assert o[0] <= NCS
        C_MASK = C_CST
        C_KDEC = C_CST + 512
        C_EPSP = C_CST + 516
        B_ID, B_OD, B_OC, B_CA, B_ST = 0, 128, 256, 384, 512

        def dma(q, out, in_, reads, writes):
            P.op(q, lambda e, out=out, in_=in_: e.dma_start(out=out, in_=in_), reads=reads, writes=writes, dma=True)

        def small_load(dst_off, src, n):
            a, d = CS(dst_off, dst_off + n)
            dma('sp', a, src, [], [d])

        small_load(C_C, cvec[:, :], 16)
        small_load(C_ADAB, adab[:, :], 176)
        small_load(C_VEC, vecs[:, :], 96)
        small_load(C_CW, convw[:, :], 248)
        small_load(C_CV, convv[:, :], 24)
        small_load(C_CST, cst[:, :], 520)
        a, d = CB(B_ID, B_ID + 128)
        dma('pool', a, identd[:, :], [], [d])
        a, d = CB(B_OD, B_OD + 128)
        P.op('dve', lambda e, a=a: e.memset(a, 1.0 / D), writes=[d])
        a, d = CB(B_OC, B_OC + 128)
        P.op('dve', lambda e, a=a: e.memset(a, 1.0 / 1024), writes=[d])
        a, d = CS(C_HALO, C_HALO + 240)
        P.op('dve', lambda e, a=a: e.memset(a, 0.0), writes=[d])
        a, d = CS(C_EPS, C_EPS + 1)
        P.op('dve', lambda e, a=a: e.memset(a, EPS), writes=[d])
        a, d = R32(0, 4096)
        P.op('dve', lambda e, a=a: e.memset(a, 0.0), writes=[d])
        a, d = RBF(0, 4096)
        P.op('dve', lambda e, a=a: e.memset(a, 0.0), writes=[d])
        epsA, epsD = CS(C_EPS, C_EPS + 1)

        ca, cd = CS(C_C, C_C + 16)
        cba, cbd = CB(B_CA, B_CA + 16)
        P.op('act', lambda e: e.activation(out=cba, in_=ca, func=AF.Silu), reads=[cd], writes=[cbd], small=True)

        slab_ctr = [0]

        def slab8():
            k = slab_ctr[0] % 4
            slab_ctr[0] += 1
            return k * 4096

        slab11_ctr = [0]

        def slab11():
            k = slab11_ctr[0] % 3
            slab11_ctr[0] += 1
            return k * 5632

        def ada_slab(s):
            so = slab8()
            sa, sd = SL(so, so + 4096)
            dma('pool', sa, adaw[s, :, :], [], [sd])
            for sub in range(2):
                col = 2 * s + sub
                pa, pd = bank(7, col, 1)
                for kc in range(16):
                    la, ld = SL(so + kc * 256 + sub * 128, so + kc * 256 + sub * 128 + 128)
                    ra, rd = CB(B_CA + kc, B_CA + kc + 1)
                    P.op('pe', lambda e, pa=pa, la=la, ra=ra, kc=kc: e.matmul(pa, la, ra, start=(kc == 0), stop=(kc == 15)),
                         reads=[ld, rd], writes=[pd])

        def mod_finalize(c0, c1):
            pa, pd = bank(7, c0, c1 - c0)
            ba, bd = CS(C_ADAB + c0, C_ADAB + c1)
            ma, md = CS(C_MOD + c0, C_MOD + c1)
            P.op('dve', lambda e: e.tensor_tensor(out=ma, in0=pa, in1=ba, op=ALU.add), reads=[pd, bd], writes=[md], small=True)
        ada_next = [0]
        for s in range(24):
            ada_slab(s)
        ada_next[0] = 24
        mod_finalize(0, 48)

        def ada_more(n):
            for _ in range(n):
                if ada_next[0] < 88:
                    ada_slab(ada_next[0])
                    ada_next[0] += 1

        def modv(k):
            return CS(C_MOD + 16 * k, C_MOD + 16 * k + 16)

        def vec(k):
            return CS(C_VEC + 16 * k, C_VEC + 16 * k + 16)

        def mk_A(dst, sc_k, n_k):
            da, dd = CS(dst, dst + 16)
            sa_, sd_ = modv(sc_k)
            na, nd = vec(n_k)
            P.op('dve', lambda e: e.scalar_tensor_tensor(out=da, in0=sa_, scalar=1.0, in1=na, op0=ALU.add, op1=ALU.mult),
                 reads=[sd_, nd], writes=[dd])

        def mk_G(dst, g_k):
            da, dd = CS(dst, dst + 16)
            ga, gd = modv(g_k)
            P.op('dve', lambda e: e.tensor_scalar(out=da, in0=ga, scalar1=0.5, scalar2=None, op0=ALU.mult),
                 reads=[gd], writes=[dd])
        mk_A(C_A1, 1, 0); mk_G(C_G1, 2)

        def mod_rest():
            ada_more(88)
            mod_finalize(48, 176)
            mk_A(C_A2, 4, 1)
            mk_A(C_A3, 7, 2); mk_G(C_G3, 8)
            mk_A(C_AF, 10, 3)
        C_B1 = C_MOD + 0
        C_B2 = C_MOD + 48
        C_G2 = C_MOD + 80
        C_B3 = C_MOD + 96
        C_BF = C_MOD + 144

        tmp_ctr = [0]

        def tmp():
            k = tmp_ctr[0] % 6
            tmp_ctr[0] += 1
            return TMP(k * 512, k * 512 + 512)

        ab_ctr = {'A': 0, 'B': 0, 'O': 0}

        def ring_bank(kind):
            base = {'A': 0, 'B': 2, 'O': 4}[kind]
            k = ab_ctr[kind] % 2
            ab_ctr[kind] += 1
            return base + k

        def Xc(c):
            return X(c * 512, c * 512 + 512)

        def Hc(c, a=0, n=512):
            return H(c * 512 + a, c * 512 + a + n)

        onesD = CB(B_OD, B_OD + 128)
        onesC = CB(B_OC, B_OC + 128)
        ident = CB(B_ID, B_ID + 128)

        def rms_norm(a_off, b_off, final=False):
            sa, sd = bank(6)
            for c in range(16):
                xa, xd = Xc(c)
                ha, hd = Hc(c)
                P.op('act', lambda e, xa=xa, ha=ha: e.activation(out=ha, in_=xa, func=AF.Square), reads=[xd], writes=[hd])
                P.op('pe', lambda e, ha=ha, c=c: e.matmul(sa, onesD[0], ha, start=(c == 0), stop=(c == 15)),
                     reads=[hd, onesD[1]], writes=[sd])
            sda, sdd = ST(512, 1024)
            P.op('act', lambda e: e.activation(out=sda, in_=sa, func=AF.Sqrt, bias=epsA, scale=1.0), reads=[sd, epsD], writes=[sdd])
            ra, rd = ST(0, 512)
            P.op('dve', lambda e: e.reciprocal(out=ra, in_=sda), reads=[sdd], writes=[rd])
            for c in range(16):
                xa, xd = Xc(c)
                ta, td = tmp()
                P.op('dve', lambda e, xa=xa, ta=ta: e.tensor_tensor(out=ta, in0=xa, in1=ra, op=ALU.mult), reads=[xd, rd], writes=[td])
                Aa, Ad = CS(a_off + c, a_off + c + 1)
                Ba, Bd = CS(b_off + c, b_off + c + 1)
                if final:
                    oa, od = scr_f(c * 2048, 512)
                else:
                    oa, od = Hc(c)
                P.op('act', lambda e, oa=oa, ta=ta, Aa=Aa, Ba=Ba: e.activation(out=oa, in_=ta, func=AF.Identity, scale=Aa, bias=Ba),
                     reads=[td, Ad, Bd], writes=[od])

        def actc(j):
            return scr_bf(j * 1024, 512)

        def ffn(k, g_off, hook=None):
            for s in range(22):
                soA = slab8()
                a_, d_ = SL(soA, soA + 4096)
                dma('pool', a_, w13[k][2 * s, :, :], [], [d_])
                soB = slab8()
                a_, d_ = SL(soB, soB + 4096)
                dma('pool', a_, w13[k][2 * s + 1, :, :], [], [d_])
                for sub in range(2):
                    j = 2 * s + sub
                    bA = ring_bank('A')
                    bB = ring_bank('B')
                    pA, dA = bank(bA)
                    pB, dB = bank(bB)
                    for (so, pp, dd) in ((soA, pA, dA), (soB, pB, dB)):
                        for kc in range(16):
                            la, ld = SL(so + kc * 256 + sub * 128, so + kc * 256 + sub * 128 + 128)
                            ha, hd = Hc(kc)
                            P.op('pe', lambda e, pp=pp, la=la, ha=ha, kc=kc: e.matmul(pp, la, ha, start=(kc == 0), stop=(kc == 15)),
                                 reads=[ld, hd], writes=[dd])
                    ta, td = tmp()
                    P.op('act', lambda e, ta=ta, pA=pA: e.activation(out=ta, in_=pA, func=AF.Silu), reads=[dA], writes=[td])
                    aa, ad = actc(j)
                    P.op('dve', lambda e, aa=aa, ta=ta, pB=pB: e.tensor_tensor(out=aa, in0=ta, in1=pB, op=ALU.mult),
                         reads=[td, dB], writes=[ad])
                if hook:
                    hook(2)
            for i in range(16):
                if hook:
                    hook(2)
                so = slab11()
                a_, d_ = SL(so, so + 5632)
                dma('pool', a_, w2[k][i, :, :], [], [d_])
                bO = ring_bank('O')
                pO, dO = bank(bO)
                for j in range(NFF):
                    la, ld = SL(so + j * 128, so + j * 128 + 128)
                    aa, ad = actc(j)
                    P.op('pe', lambda e, pO=pO, la=la, aa=aa, j=j: e.matmul(pO, la, aa, start=(j == 0), stop=(j == NFF - 1)),
                         reads=[ld, ad], writes=[dO])
                xa, xd = Xc(i)
                ga, gd = CS(g_off + i, g_off + i + 1)
                P.op('dve', lambda e, xa=xa, pO=pO, ga=ga: e.scalar_tensor_tensor(out=xa, in0=pO, scalar=ga, in1=xa, op0=ALU.mult, op1=ALU.add),
                     reads=[dO, gd, xd], writes=[xd])

        def proj_fm(so, sub, pp, dd, nk=16, rhs_fn=None):
            for kc in range(nk):
                la, ld = SL(so + kc * 256 + sub * 128, so + kc * 256 + sub * 128 + 128)
                ha, hd = rhs_fn(kc) if rhs_fn else Hc(kc)
                P.op('pe', lambda e, pp=pp, la=la, ha=ha, kc=kc, nk=nk: e.matmul(pp, la, ha, start=(kc == 0), stop=(kc == nk - 1)),
                     reads=[ld, hd], writes=[dd])

        def load_slab(src):
            so = slab8()
            n = 4096
            a_, d_ = SL(so, so + n)
            dma('pool', a_, src, [], [d_])
            return so

        oTv_dbg = outT.rearrange("(c p) t -> p c t", p=128)

        def dump(byte0, nchunks, colbase):
            for c in range(nchunks):
                a_, d_ = scr_bf(byte0 + c * 1024, 512)
                dma('pool', oTv_dbg[:, c, colbase:colbase + 512], a_, [d_], [])

        GAM = [1.0 - 2.0 ** (-5.0 - h) for h in range(4)]
        sm_ctr = [0]

        def mixer(t, mstop=None):
            t0 = t * T
            a_, d_ = TAB(0, 512)
            dma('sp', a_, cosT[:, t0:t0 + T], [], [d_])
            a_, d_ = TAB(512, 1024)
            dma('sp', a_, sinT[:, t0:t0 + T], [], [d_])
            cosA, cosD = TAB(0, 512)
            sinA, sinD = TAB(512, 1024)
            s1a, s1d = bank(4)
            s2a, s2d = bank(5)
            for s in range(4):
                soA = load_slab(win[s, :, :])
                soB = load_slab(win[4 + s, :, :])
                for sub in range(2):
                    cc = 2 * s + sub
                    pA, dA = bank(ring_bank('A'))
                    pB, dB = bank(ring_bank('B'))
                    proj_fm(soA, sub, pA, dA)
                    proj_fm(soB, sub, pB, dB)
                    ta, td = tmp()
                    P.op('act', lambda e, ta=ta, pB=pB: e.activation(out=ta, in_=pB, func=AF.Sigmoid), reads=[dB], writes=[td])
                    yo = (cc % 2) * 544
                    ha_, hd_ = CS(C_HALO + cc * 30, C_HALO + cc * 30 + 30)
                    y0a, y0d = YB(yo, yo + 30)
                    P.op('dve', lambda e, y0a=y0a, ha_=ha_: e.tensor_copy(out=y0a, in_=ha_), reads=[hd_], writes=[y0d])
                    y1a, y1d = YB(yo + 30, yo + 542)
                    P.op('dve', lambda e, y1a=y1a, pA=pA, ta=ta: e.tensor_tensor(out=y1a, in0=pA, in1=ta, op=ALU.mult),
                         reads=[dA, td], writes=[y1d], small=True)
                    yta, ytd = YB(yo + 512, yo + 542)
                    P.op('dve', lambda e, ha_=ha_, yta=yta: e.tensor_copy(out=ha_, in_=yta), reads=[ytd], writes=[hd_])
                    ca_, cd_ = scr_f(cc * 2048, 512)
                    for j in range(31):
                        ya, yd = YB(yo + j, yo + j + 512)
                        wa, wd = CS(C_CW + cc * 31 + j, C_CW + cc * 31 + j + 1)
                        if j == 0:
                            ba_, bd_ = CS(C_CV + cc, C_CV + cc + 1)
                            P.op('dve', lambda e, ca_=ca_, ya=ya, wa=wa, ba_=ba_: e.tensor_scalar(out=ca_, in0=ya, scalar1=wa, scalar2=ba_, op0=ALU.mult, op1=ALU.add),
                                 reads=[yd, wd, bd_], writes=[cd_])
                        else:
                            P.op('dve', lambda e, ca_=ca_, ya=ya, wa=wa: e.scalar_tensor_tensor(out=ca_, in0=ya, scalar=wa, in1=ca_, op0=ALU.mult, op1=ALU.add),
                                 reads=[yd, wd, cd_], writes=[cd_])
                    cba_, cbd_ = scr_bf(49152 + cc * 2048, 512)
                    csa_, csd_ = scr_bf(49152 + cc * 2048 + 1024, 512)
                    P.op('act', lambda e, cba_=cba_, ca_=ca_: e.activation(out=cba_, in_=ca_, func=AF.Identity), reads=[cd_], writes=[cbd_])
                    P.op('act', lambda e, csa_=csa_, ca_=ca_: e.activation(out=csa_, in_=ca_, func=AF.Square), reads=[cd_], writes=[csd_])
            for cc in range(8):
                cba_, cbd_ = scr_bf(49152 + cc * 2048, 512)
                csa_, csd_ = scr_bf(49152 + cc * 2048 + 1024, 512)
                P.op('pe', lambda e, cba_=cba_, cc=cc: e.matmul(s1a, onesC[0], cba_, start=(cc == 0), stop=(cc == 7)),
                     reads=[cbd_, onesC[1]], writes=[s1d])
                P.op('pe', lambda e, csa_=csa_, cc=cc: e.matmul(s2a, onesC[0], csa_, start=(cc == 0), stop=(cc == 7)),
                     reads=[csd_, onesC[1]], writes=[s2d])
            msa, msd = tmp()
            P.op('act', lambda e: e.activation(out=msa, in_=s1a, func=AF.Square), reads=[s1d], writes=[msd])
            vra, vrd = tmp()
            P.op('dve', lambda e: e.tensor_tensor(out=vra, in0=s2a, in1=msa, op=ALU.subtract), reads=[s2d, msd], writes=[vrd])
            sda, sdd = ST(512, 1024)
            P.op('act', lambda e: e.activation(out=sda, in_=vra, func=AF.Sqrt, bias=epsA, scale=1.0), reads=[vrd, epsD], writes=[sdd])
            ra, rd = ST(0, 512)
            P.op('dve', lambda e: e.reciprocal(out=ra, in_=sda), reads=[sdd], writes=[rd])
            nma, nmd = ST(1024, 1536)
            P.op('dve', lambda e: e.scalar_tensor_tensor(out=nma, in0=s1a, scalar=-1.0, in1=ra, op0=ALU.mult, op1=ALU.mult),
                 reads=[s1d, rd], writes=[nmd])
            for cc in range(8):
                ca_, cd_ = scr_f(cc * 2048, 512)
                t1a, t1d = tmp()
                P.op('dve', lambda e, t1a=t1a, ca_=ca_: e.tensor_tensor(out=t1a, in0=ca_, in1=ra, op=ALU.mult), reads=[cd_, rd], writes=[t1d])
                t2a, t2d = tmp()
                P.op('dve', lambda e, t2a=t2a, t1a=t1a: e.tensor_tensor(out=t2a, in0=t1a, in1=nma, op=ALU.add), reads=[t1d, nmd], writes=[t2d])
                ga, gd = CS(C_CV + 8 + cc, C_CV + 8 + cc + 1)
                ba_, bd_ = CS(C_CV + 16 + cc, C_CV + 16 + cc + 1)
                za, zd = scr_bf(16384 + cc * 1024, 512)
                P.op('act', lambda e, za=za, t2a=t2a, ga=ga, ba_=ba_: e.activation(out=za, in_=t2a, func=AF.Silu, scale=ga, bias=ba_),
                     reads=[t2d, gd, bd_], writes=[zd])

            if DBG_DUMPS == 1:
                dump(16384, 8, 512)
            if mstop == 'mA':
                return
            def qT(c, a=0, n=512):
                return scr_bf(24576 + c * 1024 + a * 2, n)

            def kT(c, a=0, n=512):
                return scr_bf(32768 + c * 1024 + a * 2, n)
            for which, base_slab, dst in (('q', 8, qT), ('k', 12, kT)):
                for h in range(4):
                    so = load_slab(win[base_slab + h, :, :])
                    p1, d1 = bank(ring_bank('A'))
                    p2, d2 = bank(ring_bank('B'))
                    proj_fm(so, 0, p1, d1)
                    proj_fm(so, 1, p2, d2)
                    t1a, t1d = tmp()
                    P.op('dve', lambda e, t1a=t1a, p1=p1: e.tensor_tensor(out=t1a, in0=p1, in1=cosA, op=ALU.mult), reads=[d1, cosD], writes=[t1d])
                    t2a, t2d = tmp()
                    P.op('dve', lambda e, t2a=t2a, p2=p2: e.tensor_tensor(out=t2a, in0=p2, in1=sinA, op=ALU.mult), reads=[d2, sinD], writes=[t2d])
                    oa, od = dst(2 * h)
                    P.op('dve', lambda e, oa=oa, t1a=t1a, t2a=t2a: e.tensor_tensor(out=oa, in0=t1a, in1=t2a, op=ALU.subtract), reads=[t1d, t2d], writes=[od])
                    t3a, t3d = tmp()
                    P.op('dve', lambda e, t3a=t3a, p1=p1: e.tensor_tensor(out=t3a, in0=p1, in1=sinA, op=ALU.mult), reads=[d1, sinD], writes=[t3d])
                    t4a, t4d = tmp()
                    P.op('dve', lambda e, t4a=t4a, p2=p2: e.tensor_tensor(out=t4a, in0=p2, in1=cosA, op=ALU.mult), reads=[d2, cosD], writes=[t4d])
                    oa, od = dst(2 * h + 1)
                    P.op('dve', lambda e, oa=oa, t3a=t3a, t4a=t4a: e.tensor_tensor(out=oa, in0=t3a, in1=t4a, op=ALU.add), reads=[t3d, t4d], writes=[od])

            if DBG_DUMPS == 1:
                dump(24576, 16, 1024)
            if mstop == 'mC1':
                return
            def ktm(blk, a, n):
                return scr_bf(40960 + blk * 2048 + a * 2, n)
            for blk in range(DBG_NBLK):
                kb = (ring_bank('A'), ring_bank('B'))
                for dc in range(8):
                    ia, idd = kT(dc, blk * 128, 128)
                    pa_, pd_ = bank(kb[dc // 4], (dc % 4) * 128, 128)
                    P.op('pe', lambda e, pa_=pa_, ia=ia: e.matmul(pa_, ia, ident[0], start=True, stop=True), reads=[idd, ident[1]], writes=[pd_])
                for h in range(0 if DBG_NOACT else 4):
                    pa_, pd_ = bank(kb[h // 2], (h % 2) * 256, 256)
                    oa, od = ktm(blk, h * 256, 256)
                    ka, kd = CS(C_KDEC + h, C_KDEC + h + 1)
                    P.op('dve', lambda e, oa=oa, pa_=pa_, ka=ka: e.tensor_scalar(out=oa, in0=pa_, scalar1=ka, scalar2=None, op0=ALU.mult), reads=[pd_, kd], writes=[od])

            if mstop == 'mC2':
                return
            def vtm(blk, a, n):
                return scr_bf(blk * 4096 + a * 2, n)
            vr = [0]
            for s in range(DBG_VS):
                so = load_slab(win[16 + s, :, :])
                for blk in range(DBG_VB):
                    r = vr[0] % 8
                    vr[0] += 1
                    pa_, pd_ = bank(r % 4, 0, 256)
                    for kc in range(16):
                        la, ld = Hc(kc, blk * 128, 128)
                        ra_, rd_ = SL(so + kc * 256, so + kc * 256 + 256)
                        P.op('pe', lambda e, pa_=pa_, la=la, ra_=ra_, kc=kc: e.matmul(pa_, la, ra_, start=(kc == 0), stop=(kc == 15)),
                             reads=[ld, rd_], writes=[pd_])
                    oa, od = vtm(blk, s * 256, 256)
                    P.op('act', lambda e, oa=oa, pa_=pa_: e.activation(out=oa, in_=pa_, func=AF.Identity), reads=[pd_], writes=[od])

            if DBG_DUMPS == 1:
                dump(0, 16, 1536)
            if mstop == 'mC3':
                return
            def ontm(blk, a, n):
                return scr_bf(49152 + blk * 4096 + a * 2, n)
            unit = 0
            for blk in range(4):
                for h in range(4):
                    psS, dS = bank(6 + (unit % 2), 0, 128)
                    for dc in range(2):
                        la, ld = kT(2 * h + dc, blk * 128, 128)
                        ra_, rd_ = qT(2 * h + dc, blk * 128, 128)
                        P.op('pe', lambda e, psS=psS, la=la, ra_=ra_, dc=dc: e.matmul(psS, la, ra_, start=(dc == 0), stop=(dc == 1)),
                             reads=[ld, rd_], writes=[dS])
                    sto = B_ST + (unit % 4) * 128
                    sTa, sTd = CB(sto, sto + 128)
                    mka, mkd = CS(C_MASK + h * 128, C_MASK + h * 128 + 128)
                    P.op('dve', lambda e, sTa=sTa, psS=psS, mka=mka: e.tensor_tensor(out=sTa, in0=psS, in1=mka, op=ALU.mult), reads=[dS, mkd], writes=[sTd])
                    pO, dO = bank(ring_bank('O'))
                    va, vd = vtm(blk, h * 512, 512)
                    P.op('pe', lambda e, pO=pO, sTa=sTa, va=va: e.matmul(pO, sTa, va, start=True, stop=False), reads=[sTd, vd], writes=[dO])
                    for dc in range(2):
                        la, ld = qT(2 * h + dc, blk * 128, 128)
                        ra_, rd_ = RBF((h * 2 + dc) * 512, (h * 2 + dc) * 512 + 512)
                        P.op('pe', lambda e, pO=pO, la=la, ra_=ra_, dc=dc: e.matmul(pO, la, ra_, start=False, stop=(dc == 1)),
                             reads=[ld, rd_], writes=[dO])
                    g128 = float(GAM[h] ** 128)
                    for dc in range(2):
                        pR, dR = bank((unit % 2) * 2 + dc)
                        la, ld = ktm(blk, h * 256 + dc * 128, 128)
                        P.op('pe', lambda e, pR=pR, la=la, va=va: e.matmul(pR, la, va, start=True, stop=True), reads=[ld, vd], writes=[dR])
                        r32a, r32d = R32((h * 2 + dc) * 512, (h * 2 + dc) * 512 + 512)
                        P.op('dve', lambda e, r32a=r32a, pR=pR, g128=g128: e.scalar_tensor_tensor(out=r32a, in0=r32a, scalar=g128, in1=pR, op0=ALU.mult, op1=ALU.add),
                             reads=[r32d, dR], writes=[r32d])
                        rba, rbd = RBF((h * 2 + dc) * 512, (h * 2 + dc) * 512 + 512)
                        P.op('act', lambda e, rba=rba, r32a=r32a: e.activation(out=rba, in_=r32a, func=AF.Identity, scale=0.0625), reads=[r32d], writes=[rbd])
                    sm = C_SM + (sm_ctr[0] % 4) * 16
                    sm_ctr[0] += 1
                    st6a, st6d = CS(sm, sm + 6)
                    mva, mvd = CS(sm + 6, sm + 8)
                    P.op('dve', lambda e, st6a=st6a, pO=pO: e.bn_stats(out=st6a, in_=pO), reads=[dO], writes=[st6d], small=True)
                    P.op('dve', lambda e, mva=mva, st6a=st6a: e.bn_aggr(out=mva, in_=st6a), reads=[st6d], writes=[mvd], small=True)
                    vea, ved = CS(sm + 8, sm + 9)
                    epa, epd = CS(C_EPSP + h, C_EPSP + h + 1)
                    vara, vard = CS(sm + 7, sm + 8)
                    mea, med = CS(sm + 6, sm + 7)
                    P.op('dve', lambda e, vea=vea, vara=vara, epa=epa: e.tensor_tensor(out=vea, in0=vara, in1=epa, op=ALU.add), reads=[mvd, epd], writes=[ved], small=True)
                    sqa, sqd = CS(sm + 9, sm + 10)
                    P.op('act', lambda e, sqa=sqa, vea=vea: e.activation(out=sqa, in_=vea, func=AF.Sqrt), reads=[ved], writes=[sqd], small=True)
                    rsa, rsd = CS(sm + 10, sm + 11)
                    P.op('dve', lambda e, rsa=rsa, sqa=sqa: e.reciprocal(out=rsa, in_=sqa), reads=[sqd], writes=[rsd], small=True)
                    nra, nrd = CS(sm + 11, sm + 12)
                    P.op('dve', lambda e, nra=nra, mea=mea, rsa=rsa: e.scalar_tensor_tensor(out=nra, in0=mea, scalar=-1.0, in1=rsa, op0=ALU.mult, op1=ALU.mult),
                         reads=[mvd, rsd], writes=[nrd], small=True)
                    ona, ond = ontm(blk, h * 512, 512)
                    P.op('dve', lambda e, ona=ona, pO=pO, rsa=rsa, nra=nra: e.tensor_scalar(out=ona, in0=pO, scalar1=rsa, scalar2=nra, op0=ALU.mult, op1=ALU.add),
                         reads=[dO, rsd, nrd], writes=[ond])
                    unit += 1

            if DBG_DUMPS == 2:
                dump(49152, 16, 512)
            if mstop == 'mC5':
                return
            def sgT(i):
                return scr_bf(i * 1024, 512)
            for ip in range(8):
                so = load_slab(win[24 + ip, :, :])
                for sub in range(2):
                    i = 2 * ip + sub
                    pp, dd = bank(ring_bank('A' if sub == 0 else 'B'))
                    proj_fm(so, sub, pp, dd)
                    oa, od = sgT(i)
                    P.op('act', lambda e, oa=oa, pp=pp: e.activation(out=oa, in_=pp, func=AF.Silu), reads=[dd], writes=[od])

            if mstop == 'mC4':
                return
            def oT(i):
                return scr_bf(24576 + i * 1024, 512)
            for ec in range(16):
                tb = ring_bank('A' if ec % 2 == 0 else 'B')
                for blk in range(4):
                    ia, idd = ontm(blk, ec * 128, 128)
                    pa_, pd_ = bank(tb, blk * 128, 128)
                    P.op('pe', lambda e, pa_=pa_, ia=ia: e.matmul(pa_, ia, ident[0], start=True, stop=True), reads=[idd, ident[1]], writes=[pd_])
                pa_, pd_ = bank(tb)
                ta, td = tmp()
                ga, gd = CS(C_VEC + 64 + ec, C_VEC + 64 + ec + 1)
                ba_, bd_ = CS(C_VEC + 80 + ec, C_VEC + 80 + ec + 1)
                P.op('dve', lambda e, ta=ta, pa_=pa_, ga=ga, ba_=ba_: e.tensor_scalar(out=ta, in0=pa_, scalar1=ga, scalar2=ba_, op0=ALU.mult, op1=ALU.add),
                     reads=[pd_, gd, bd_], writes=[td])
                sa_, sd_ = sgT(ec)
                oa, od = oT(ec)
                P.op('dve', lambda e, oa=oa, ta=ta, sa_=sa_: e.tensor_tensor(out=oa, in0=ta, in1=sa_, op=ALU.mult), reads=[td, sd_], writes=[od])

            if DBG_DUMPS == 2:
                dump(24576, 16, 1024)
            if mstop == 'mC6':
                return
            def zc(kc):
                return scr_bf(16384 + kc * 1024, 512)

            def mg(i):
                return scr_bf(i * 1024, 512)
            for ip in range(8):
                so = load_slab(win[32 + ip, :, :])
                sg = []
                for sub in range(2):
                    pp, dd = bank(ring_bank('A'))
                    proj_fm(so, sub, pp, dd)
                    ta, td = TMP(sub * 512, sub * 512 + 512)
                    P.op('act', lambda e, ta=ta, pp=pp: e.activation(out=ta, in_=pp, func=AF.Sigmoid), reads=[dd], writes=[td])
                    sg.append((ta, td))
                so = slab8()
                a_, d_ = SL(so, so + 2048)
                dma('pool', a_, pw[ip, :, :], [], [d_])
                for sub in range(2):
                    pp, dd = bank(ring_bank('B'))
                    proj_fm(so, sub, pp, dd, nk=8, rhs_fn=zc)
                    ta, td = sg[sub]
                    P.op('dve', lambda e, ta=ta, pp=pp: e.tensor_tensor(out=ta, in0=ta, in1=pp, op=ALU.mult), reads=[td, dd], writes=[td])
                so = load_slab(win[40 + ip, :, :])
                sr = []
                for sub in range(2):
                    pp, dd = bank(ring_bank('A'))
                    proj_fm(so, sub, pp, dd)
                    ta, td = TMP((2 + sub) * 512, (2 + sub) * 512 + 512)
                    P.op('act', lambda e, ta=ta, pp=pp: e.activation(out=ta, in_=pp, func=AF.Sigmoid), reads=[dd], writes=[td])
                    sr.append((ta, td))
                so = load_slab(wo[ip, :, :])
                for sub in range(2):
                    i = 2 * ip + sub
                    pp, dd = bank(ring_bank('B'))
                    proj_fm(so, sub, pp, dd, nk=16, rhs_fn=oT)
                    ta, td = sr[sub]
                    P.op('dve', lambda e, ta=ta, pp=pp: e.tensor_tensor(out=ta, in0=ta, in1=pp, op=ALU.mult), reads=[td, dd], writes=[td])
                    ca_, cd_ = sg[sub]
                    oa, od = mg(i)
                    P.op('dve', lambda e, oa=oa, ta=ta, ca_=ca_: e.tensor_tensor(out=oa, in0=ta, in1=ca_, op=ALU.add), reads=[td, cd_], writes=[od])
            tmp_ctr[0] = 4
            if DBG_DUMPS == 2:
                dump(0, 16, 1536)
            if mstop == 'mD':
                return
            for ip in range(8):
                so = load_slab(wout[ip, :, :])
                for sub in range(2):
                    i2 = 2 * ip + sub
                    pO, dO = bank(ring_bank('O'))
                    proj_fm(so, sub, pO, dO, nk=16, rhs_fn=mg)
                    xa, xd = Xc(i2)
                    ga, gd = CS(C_G2 + i2, C_G2 + i2 + 1)
                    P.op('dve', lambda e, xa=xa, pO=pO, ga=ga: e.scalar_tensor_tensor(out=xa, in0=pO, scalar=ga, in1=xa, op0=ALU.mult, op1=ALU.add),
                         reads=[dO, gd, xd], writes=[xd])

        X3 = X.t[:, :].rearrange("p (c t) -> p c t", t=512)
        xTv = xT.rearrange("(c p) t -> p c t", p=128)
        oTv = outT.rearrange("(c p) t -> p c t", p=128)
        for t in range(n_tiles):
            t0 = t * T
            dma('sp', X3, xTv[:, :, t0:t0 + T], [], [('X', 0, 32768)])
            done = False
            rms_norm(C_A1, C_B1)
            ffn(0, C_G1, hook=(ada_more if t == 0 else None))
            if t == 0:
                mod_rest()
            if stop != 'ffn1':
                rms_norm(C_A2, C_B2)
                mixer(t, stop)
                if stop == 'mix' or (stop is not None and stop.startswith('m')):
                    pass
                else:
                    rms_norm(C_A3, C_B3)
                    ffn(1, C_G3)
                    if stop != 'ffn2':
                        rms_norm(C_AF, C_BF, final=True)
                        O3 = SCRt[:, 0:8192].rearrange("p (c t) -> p c t", t=512)
                        dma('sp', oTv[:, :, t0:t0 + T], O3, [('SCR', 0, 32768)], [])
                        done = True
            if not done:
                dma('sp', oTv[:, :, t0:t0 + T], X3, [('X', 0, 32768)], [])

        P.emit(nc, block, sems, rings)
    nc._prog = P
    return nc


def _slabs(w, F):
    K, N = w.shape
    a = w.reshape(K // 128, 128, N // F, F).transpose(2, 1, 0, 3)
    return np.ascontiguousarray(a).reshape(N // F, 128, (K // 128) * F)


def _w2_slabs(w):
    a = w.reshape(NFF, 128, 16, 128).transpose(2, 1, 0, 3)
    return np.ascontiguousarray(a).reshape(16, 128, NFF * 128)


def _pv(v):
    return np.ascontiguousarray(v.reshape(-1, 128).T)


def _prep_shared(inp):
    sh = {}
    f = lambda k: np.asarray(inp[k], dtype=np.float32)
    sh["adaw"] = _slabs(np.concatenate([f("ada_w")[0], f("ada_f_w")], axis=1), 256)
    sh["adab"] = np.concatenate([_pv(f("ada_b")[0]), _pv(f("ada_f_b"))], axis=1)
    sh["vecs"] = np.concatenate([_pv(f("ffn1_norm")[0]), _pv(f("mix_norm")[0]), _pv(f("ffn2_norm")[0]),
                                 _pv(f("final_norm")), _pv(f("ret_gn_g")[0]), _pv(f("ret_gn_b")[0])], axis=1)
    cw = f("conv_dw_w")[0]
    sh["convw"] = np.ascontiguousarray(cw.reshape(31, 8, 128).transpose(2, 1, 0)).reshape(128, 248)
    sh["convv"] = np.concatenate([_pv(f("conv_dw_b")[0]), _pv(f("conv_ln_g")[0]), _pv(f("conv_ln_b")[0])], axis=1)
    for k, (a, b, c) in enumerate((("ffn1_w1", "ffn1_w3", "ffn1_w2"), ("ffn2_w1", "ffn2_w3", "ffn2_w2"))):
        s1 = _slabs(f(a)[0], 256)
        s3 = _slabs(f(b)[0], 256)
        sh["w13_%d" % k] = np.ascontiguousarray(np.stack([s1, s3], axis=1)).reshape(44, 128, 4096)
        sh["w2_%d" % k] = _w2_slabs(f(c)[0])
    sh["win"] = _slabs(f("w_in")[0], 256)
    sh["pw"] = _slabs(f("conv_pw_w")[0], 256)
    sh["wo"] = _slabs(f("ret_w_o")[0], 256)
    sh["wout"] = _slabs(f("w_out")[0], 256)
    pos = np.arange(S, dtype=np.float32)
    inv_freq = (np.float32(10000.0) ** (-np.arange(0, 256, 2, dtype=np.float32) / np.float32(256))).astype(np.float32)
    ang = (pos[:, None] * inv_freq[None, :]).astype(np.float32)
    sh["cosT"] = np.ascontiguousarray(np.cos(ang).T.astype(np.float32))
    sh["sinT"] = np.ascontiguousarray(np.sin(ang).T.astype(np.float32))
    cstv = np.zeros((128, 520), dtype=np.float32)
    pidx = np.arange(128, dtype=np.float64)
    for h in range(4):
        gam = 1.0 - 2.0 ** (-5.0 - h)
        m = pidx[:, None]
        c = pidx[None, :]
        same = (m // 64) == (c // 64)
        causal_x = ((m // 64) == 0) & ((c // 64) == 1)
        wgt = np.where(same, gam ** np.abs(c - m), np.where(causal_x, gam ** (c - m), 0.0))
        s = gam ** (c + 1.0)
        cstv[:, h * 128:(h + 1) * 128] = (wgt / s / 16.0).astype(np.float32)
        cstv[:, 512 + h] = (gam ** (127.0 - pidx)).astype(np.float32)
        cstv[:, 516 + h] = (EPS / (gam ** (2.0 * (pidx + 1.0)))).astype(np.float32)
    sh["cst"] = cstv
    sh["identd"] = np.eye(128, dtype=np.float32)
    return sh


_NC_CACHE = {}


def kernel(**inputs):
    x = np.asarray(inputs["x"], dtype=np.float32)
    c = np.asarray(inputs["c"], dtype=np.float32)
    shared = _prep_shared(inputs)
    key = (NT if DBG_TILES is None else DBG_TILES, STOP)
    if key not in _NC_CACHE:
        _NC_CACHE[key] = build_program(n_tiles=key[0], stop=key[1])
    nc = _NC_CACHE[key]
    in_maps = []
    for b in range(N_CORES):
        m = dict(shared)
        m["xT"] = np.ascontiguousarray(x[b].T)
        m["cvec"] = _pv(c[b])
        in_maps.append(m)
    res = run_bass_kernel_spmd(nc, in_maps, core_ids=list(range(N_CORES)))
    out = np.empty((N_CORES, S, D), dtype=np.float32)
    for b in range(N_CORES):
        out[b] = res.results[b]["outT"].T
    return out
```
